# Optimizing a Trainium2 kernel written in Bass

```python
import jax, jax.numpy as jnp
from jax import lax
import numpy as np

D_MODEL = 2048
BATCH = 4
SEQ = 2048
DEPTH = 1
DEC_BATCH = 128
DEC_SEQ = 8
PAST_LEN = 16384
PAGE_SIZE = 128

HG_HEADS = 16
HG_DK = 128
HG_DV = D_MODEL // HG_HEADS
HG_KDIM = HG_HEADS * HG_DK
HG_VDIM = HG_HEADS * HG_DV
M_INNER = D_MODEL
M_HEADDIM = 64
M_HEADS = M_INNER // M_HEADDIM
M_GROUPS = 4
M_HPG = M_HEADS // M_GROUPS
M_DSTATE = 128
M_CONV = 4
M_CONV_DIM = M_INNER + 2 * M_GROUPS * M_DSTATE
N_BRANCH = 2
FFN_HIDDEN = -(-8 * D_MODEL // (3 * 256)) * 256
CHUNK = 64
EPS = 1e-6
IN_WIDTHS = (HG_KDIM, HG_KDIM, HG_VDIM, HG_VDIM, M_INNER, M_CONV_DIM, M_HEADS, N_BRANCH * D_MODEL)
W_IN_WIDTH = sum(IN_WIDTHS)

kernel_name = "hgrn2_mamba2_gated_hybrid_step"


def rmsnorm(x, w):
    xf = x.astype(jnp.float32)
    y = xf * lax.rsqrt(jnp.mean(xf * xf, axis=-1, keepdims=True) + EPS)
    return (y * w.astype(jnp.float32)).astype(x.dtype)


def _pad_time(a, n_pad):
    return jnp.pad(a, [(0, 0), (0, n_pad)] + [(0, 0)] * (a.ndim - 2))


def _to_chunks(a, nc, c):
    return jnp.moveaxis(a.reshape(a.shape[0], nc, c, *a.shape[2:]), 1, 0)


def _from_chunks(o, T):
    nc, B, c = o.shape[:3]
    return jnp.moveaxis(o, 0, 1).reshape(B, nc * c, *o.shape[3:])[:, :T]


def hgrn2_chunked(q, k, v, log_f, s0):
    T = q.shape[1]
    c = min(CHUNK, T)
    n_pad = (-T) % c
    q, k, v, log_f = (_pad_time(a, n_pad) for a in (q, k, v, log_f))
    nc = (T + n_pad) // c
    mask = jnp.tril(jnp.ones((c, c), dtype=bool))[None, :, :, None, None]

    def step(S, blk):
        qc, kc, vc, gc = blk
        b = jnp.cumsum(gc, axis=1)
        decay = jnp.exp(jnp.where(mask, b[:, :, None] - b[:, None, :], -jnp.inf))
        scores = jnp.einsum('bthd,bshd,btshd->bhts', qc, kc, decay)
        o = (jnp.einsum('bhts,bshv->bthv', scores, vc)
             + jnp.einsum('bthd,bhdv->bthv', qc * jnp.exp(b), S))
        b_last = b[:, -1]
        S = (jnp.exp(b_last)[..., None] * S
             + jnp.einsum('bshd,bshv->bhdv', kc * jnp.exp(b_last[:, None] - b), vc))
        return S, o

    S, o = lax.scan(step, s0, (_to_chunks(q, nc, c), _to_chunks(k, nc, c),
                              _to_chunks(v, nc, c), _to_chunks(log_f, nc, c)))
    return _from_chunks(o, T), S


def ssd_chunked(x, dt, A, Bm, Cm, s0):
    T = x.shape[1]
    c = min(CHUNK, T)
    n_pad = (-T) % c
    x, dt, Bm, Cm = (_pad_time(a, n_pad) for a in (x, dt, Bm, Cm))
    nc = (T + n_pad) // c
    mask = jnp.tril(jnp.ones((c, c), dtype=bool))[None, :, :, None, None]

    def step(S, blk):
        xc, dtc, bc, cc = blk
        acum = jnp.cumsum(dtc * A, axis=1)
        L = jnp.exp(jnp.where(mask, acum[:, :, None] - acum[:, None, :], -jnp.inf))
        cb = jnp.einsum('btgn,bsgn->bgts', cc, bc)
        y = (jnp.einsum('bgts,btsgr,bsgr,bsgrp->btgrp', cb, L, dtc, xc)
             + jnp.einsum('btgn,bgrpn,btgr->btgrp', cc, S, jnp.exp(acum)))
        a_last = acum[:, -1]
        S = (jnp.exp(a_last)[..., None, None] * S
             + jnp.einsum('bsgn,bsgr,bsgrp->bgrpn', bc, jnp.exp(a_last[:, None] - acum) * dtc, xc))
        return S, y

    S, y = lax.scan(step, s0, (_to_chunks(x, nc, c), _to_chunks(dt, nc, c),
                              _to_chunks(Bm, nc, c), _to_chunks(Cm, nc, c)))
    return _from_chunks(y, T), S


def causal_conv(xbc, buf, conv_w, conv_b):
    T = xbc.shape[1]
    xp = jnp.concatenate([buf.astype(xbc.dtype), xbc], axis=1)
    out = conv_b + sum(xp[:, j:j + T] * conv_w[j] for j in range(M_CONV))
    return jax.nn.silu(out), xp[:, -(M_CONV - 1):]


def _split_proj(proj):
    idx = np.cumsum(IN_WIDTHS)[:-1].tolist()
    return jnp.split(proj, idx, axis=-1)


def hgrn2_mixer(q, f_raw, v, g, lb, hg_norm, s0):
    B, T, _ = q.shape
    f32 = jnp.float32
    sig = jax.nn.sigmoid(f_raw.astype(f32))
    f = lb + (1.0 - lb) * sig
    log_f = jnp.log(f)
    k = (1.0 - lb) * jax.nn.sigmoid(-f_raw.astype(f32))
    qh = (q.astype(f32) * HG_DK ** -0.5).reshape(B, T, HG_HEADS, HG_DK)
    kh = k.reshape(B, T, HG_HEADS, HG_DK)
    gh = log_f.reshape(B, T, HG_HEADS, HG_DK)
    vh = v.astype(f32).reshape(B, T, HG_HEADS, HG_DV)
    o, s_new = hgrn2_chunked(qh, kh, vh, gh, s0.astype(f32))
    o = rmsnorm(o, hg_norm.reshape(HG_HEADS, HG_DV)).reshape(B, T, HG_VDIM)
    o = o * jax.nn.silu(g.astype(f32))
    return o.astype(q.dtype), s_new.astype(s0.dtype)


def mamba2_mixer(z, xbc, dt_raw, conv_buf, conv_w, conv_b, dt_bias, a_log, d_skip, ssm_norm, s0):
    B, T, _ = z.shape
    f32 = jnp.float32
    xbc_c, new_buf = causal_conv(xbc, conv_buf, conv_w, conv_b)
    xs, Bm, Cm = jnp.split(xbc_c, [M_INNER, M_INNER + M_GROUPS * M_DSTATE], axis=-1)
    xh = xs.astype(f32).reshape(B, T, M_GROUPS, M_HPG, M_HEADDIM)
    Bm = Bm.astype(f32).reshape(B, T, M_GROUPS, M_DSTATE)
    Cm = Cm.astype(f32).reshape(B, T, M_GROUPS, M_DSTATE)
    dt = jax.nn.softplus(dt_raw.astype(f32) + dt_bias.astype(f32)).reshape(B, T, M_GROUPS, M_HPG)
    A = -jnp.exp(a_log.astype(f32)).reshape(M_GROUPS, M_HPG)
    s0g = s0.astype(f32).reshape(B, M_GROUPS, M_HPG, M_HEADDIM, M_DSTATE)
    y, s_new = ssd_chunked(xh, dt, A, Bm, Cm, s0g)
    y = y + d_skip.astype(f32).reshape(M_GROUPS, M_HPG)[:, :, None] * xh
    y = y.reshape(B, T, M_INNER) * jax.nn.silu(z.astype(f32))
    y = rmsnorm(y.reshape(B, T, M_GROUPS, M_INNER // M_GROUPS),
                ssm_norm.reshape(M_GROUPS, M_INNER // M_GROUPS)).reshape(B, T, M_INNER)
    s_new = s_new.reshape(B, M_HEADS, M_HEADDIM, M_DSTATE)
    return y.astype(z.dtype), s_new.astype(s0.dtype), new_buf.astype(conv_buf.dtype)


def _layer(x, s_hg, s_ssm, conv_buf, lb, norm_mix, w_in, hg_norm, conv_w, conv_b, dt_bias,
           a_log, d_skip, ssm_norm, w_branch_hg, w_branch_ssm, w_out, norm_ffn,
           w_ffn_gate, w_ffn_up, w_ffn_down):
    B, T, _ = x.shape
    h = rmsnorm(x, norm_mix)
    proj = jnp.einsum('btd,de->bte', h, w_in)
    q, f_raw, v, g, z, xbc, dt_raw, gate_raw = _split_proj(proj)
    o_hg, s_hg_new = hgrn2_mixer(q, f_raw, v, g, lb, hg_norm, s_hg)
    o_m, s_ssm_new, buf_new = mamba2_mixer(z, xbc, dt_raw, conv_buf, conv_w, conv_b, dt_bias,
                                           a_log, d_skip, ssm_norm, s_ssm)
    gates = jax.nn.sigmoid(gate_raw.astype(jnp.float32)).reshape(B, T, N_BRANCH, D_MODEL)
    y_hg = jnp.einsum('bte,ed->btd', o_hg, w_branch_hg).astype(jnp.float32)
    y_m = jnp.einsum('bte,ed->btd', o_m, w_branch_ssm).astype(jnp.float32)
    merged = (gates[:, :, 0] * y_hg + gates[:, :, 1] * y_m).astype(x.dtype)
    x = x + jnp.einsum('btd,de->bte', merged, w_out)
    h2 = rmsnorm(x, norm_ffn)
    act = jax.nn.silu(jnp.einsum('btd,df->btf', h2, w_ffn_gate)) * jnp.einsum('btd,df->btf', h2, w_ffn_up)
    x = x + jnp.einsum('btf,fd->btd', act, w_ffn_down)
    return x, s_hg_new, s_ssm_new, buf_new


def setup_inputs(seed: int = 0) -> dict:
    key = jax.random.key(seed)
    ks = jax.random.split(key, 24)
    f32 = jnp.float32
    nrm = lambda k, shape, s: jax.random.normal(k, shape, f32) * s
    dt0 = jnp.exp(jax.random.uniform(ks[10], (DEPTH, M_HEADS), f32,
                                     float(np.log(1e-3)), float(np.log(1e-1))))
    return {
        'x_prompt': nrm(ks[0], (BATCH, SEQ, D_MODEL), 1.0),
        'x_sample': nrm(ks[1], (DEC_BATCH, DEC_SEQ, D_MODEL), 1.0),
        'state_hgrn': nrm(ks[2], (DEPTH, DEC_BATCH, HG_HEADS, HG_DK, HG_DV), 0.5),
        'state_ssm': nrm(ks[3], (DEPTH, DEC_BATCH, M_HEADS, M_HEADDIM, M_DSTATE), 0.1),
        'state_conv': nrm(ks[4], (DEPTH, DEC_BATCH, M_CONV - 1, M_CONV_DIM), 1.0),
        'norm_mix': 1.0 + nrm(ks[5], (DEPTH, D_MODEL), 0.02),
        'w_in': nrm(ks[6], (DEPTH, D_MODEL, W_IN_WIDTH), D_MODEL ** -0.5),
        'hg_lb_logits': nrm(ks[7], (DEPTH + 1, HG_KDIM), 0.5),
        'hg_norm': 1.0 + nrm(ks[8], (DEPTH, HG_VDIM), 0.02),
        'conv_w': nrm(ks[9], (DEPTH, M_CONV, M_CONV_DIM), M_CONV ** -0.5),
        'conv_b': nrm(ks[11], (DEPTH, M_CONV_DIM), 0.02),
        'dt_bias': dt0 + jnp.log(-jnp.expm1(-dt0)),
        'a_log': jnp.log(jax.random.uniform(ks[12], (DEPTH, M_HEADS), f32, 1.0, 16.0)),
        'd_skip': 1.0 + nrm(ks[13], (DEPTH, M_HEADS), 0.02),
        'ssm_norm': 1.0 + nrm(ks[14], (DEPTH, M_INNER), 0.02),
        'w_branch_hg': nrm(ks[15], (DEPTH, HG_VDIM, D_MODEL), HG_VDIM ** -0.5),
        'w_branch_ssm': nrm(ks[16], (DEPTH, M_INNER, D_MODEL), M_INNER ** -0.5),
        'w_out': nrm(ks[17], (DEPTH, D_MODEL, D_MODEL), D_MODEL ** -0.5),
        'norm_ffn': 1.0 + nrm(ks[18], (DEPTH, D_MODEL), 0.02),
        'w_ffn_gate': nrm(ks[19], (DEPTH, D_MODEL, FFN_HIDDEN), D_MODEL ** -0.5),
        'w_ffn_up': nrm(ks[20], (DEPTH, D_MODEL, FFN_HIDDEN), D_MODEL ** -0.5),
        'w_ffn_down': nrm(ks[21], (DEPTH, FFN_HIDDEN, D_MODEL), FFN_HIDDEN ** -0.5),
        'norm_final': 1.0 + nrm(ks[22], (D_MODEL,), 0.02),
    }


def reference(x_prompt, x_sample, state_hgrn, state_ssm, state_conv, norm_mix, w_in, hg_lb_logits,
              hg_norm, conv_w, conv_b, dt_bias, a_log, d_skip, ssm_norm, w_branch_hg, w_branch_ssm,
              w_out, norm_ffn, w_ffn_gate, w_ffn_up, w_ffn_down, norm_final):
    lb_all = jnp.cumsum(jax.nn.softmax(hg_lb_logits.astype(jnp.float32), axis=0), axis=0)
    xp, xs = x_prompt, x_sample
    hp_l, sp_l, cp_l, hs_l, ss_l, cs_l = [], [], [], [], [], []
    for l in range(DEPTH):
        wl = (norm_mix[l], w_in[l], hg_norm[l], conv_w[l], conv_b[l], dt_bias[l], a_log[l],
              d_skip[l], ssm_norm[l], w_branch_hg[l], w_branch_ssm[l], w_out[l], norm_ffn[l],
              w_ffn_gate[l], w_ffn_up[l], w_ffn_down[l])
        s_hg0 = jnp.zeros((BATCH, HG_HEADS, HG_DK, HG_DV), x_prompt.dtype)
        s_ssm0 = jnp.zeros((BATCH, M_HEADS, M_HEADDIM, M_DSTATE), x_prompt.dtype)
        buf0 = jnp.zeros((BATCH, M_CONV - 1, M_CONV_DIM), x_prompt.dtype)
        xp, hp, sp, cp = _layer(xp, s_hg0, s_ssm0, buf0, lb_all[l], *wl)
        xs, hs, ss, cs = _layer(xs, state_hgrn[l], state_ssm[l], state_conv[l], lb_all[l], *wl)
        hp_l.append(hp); sp_l.append(sp); cp_l.append(cp)
        hs_l.append(hs); ss_l.append(ss); cs_l.append(cs)
    y_prompt = rmsnorm(xp, norm_final)
    y_sample = rmsnorm(xs, norm_final)
    return (y_prompt, y_sample, jnp.stack(hp_l), jnp.stack(sp_l), jnp.stack(cp_l),
            jnp.stack(hs_l), jnp.stack(ss_l), jnp.stack(cs_l))
```

```python
import numpy as np
import ml_dtypes
from contextlib import ExitStack
import concourse.bass as bass
import concourse.mybir as mybir
from concourse.bass_utils import run_bass_kernel_spmd

F32 = mybir.dt.float32
BF16 = mybir.dt.bfloat16
AF = mybir.ActivationFunctionType
ALU = mybir.AluOpType

D = 2048
NCH = 16
TP = 1024
TM = 1024
TS = 128
NM = TM + TS
FFN = 5632
EPS = 1e-6
OQ, OF, OV, OG, OZ, OX, OB, OC, ODT, OG0, OG1 = 0, 2048, 4096, 6144, 8192, 10240, 12288, 12800, 13312, 13344, 15392
WIN = 17440
NCORES = 8

C_NMIX, C_L0, C_L1, C_HGN, C_SSMN, C_NFFN, C_CW, C_CB, C_DSK, C_DTB, C_ALOG, C_FLAG = 0, 16, 32, 48, 64, 80, 96, 192, 216, 232, 233, 234
NCOLS = 235


class Prog:
    ENGS = ("pe", "act", "dve", "pool", "sp")

    def __init__(self):
        self.ops = []

    def op(self, eng, fn, reads=(), writes=(), dma=None):
        self.ops.append(dict(eng=eng, fn=fn, reads=list(reads), writes=list(writes), dma=dma, bar=False))

    def pe(self, fn, r=(), w=()):
        self.op("pe", fn, r, w)

    def act(self, fn, r=(), w=()):
        self.op("act", fn, r, w)

    def dve(self, fn, r=(), w=()):
        self.op("dve", fn, r, w)

    def pool(self, fn, r=(), w=()):
        self.op("pool", fn, r, w)

    def dma(self, eng, sem, fn, r=(), w=()):
        self.op(eng, fn, r, w, dma=sem)

    def dma_multi(self, eng, sem, fns, r=(), w=()):
        self.op(eng, list(fns), r, w, dma=sem)

    def capture(self, fn):
        n0 = len(self.ops)
        fn()
        got = self.ops[n0:]
        del self.ops[n0:]
        return got

    def interleave(self, a, b):
        ia = ib = 0
        na, nb = len(a), len(b)
        while ia < na or ib < nb:
            if ib >= nb or (ia < na and ia * nb <= ib * na):
                self.ops.append(a[ia])
                ia += 1
            else:
                self.ops.append(b[ib])
                ib += 1

    def barrier(self):
        for e in ("act", "dve", "pool", "sp", "pe"):
            self.ops.append(dict(eng=e, fn=None, reads=[], writes=[], dma=None, bar=True))

    def analyze(self):
        ops = self.ops
        last_w = {}
        readers = {}
        last_eng = {}
        open_dma = []
        for i, o in enumerate(ops):
            deps = set()
            if o["bar"]:
                for e, j in last_eng.items():
                    deps.add(j)
                deps.update(open_dma)
            for k in o["reads"]:
                if k in last_w:
                    deps.add(last_w[k])
            for k in o["writes"]:
                if k in last_w:
                    deps.add(last_w[k])
                for r in readers.get(k, ()):
                    deps.add(r)
            deps.discard(i)
            o["deps"] = deps
            for k in o["reads"]:
                readers.setdefault(k, []).append(i)
            for k in o["writes"]:
                last_w[k] = i
                readers[k] = []
            if o["fn"] is not None:
                if o["dma"] is not None:
                    open_dma.append(i)
                else:
                    last_eng[o["eng"]] = i
        need = [False] * len(ops)
        for i, o in enumerate(ops):
            for j in o["deps"]:
                dj = ops[j]
                if dj["dma"] is None and dj["eng"] == "pe" and o["eng"] == "pe" and o["dma"] is None and not o["bar"]:
                    continue
                need[j] = True
        cnt = {}
        clock = {e: {} for e in self.ENGS}
        for i, o in enumerate(ops):
            E = o["eng"]
            clk = clock[E]
            waits = {}
            for j in sorted(o["deps"]):
                dj = ops[j]
                if dj["dma"] is None and dj["eng"] == "pe" and E == "pe" and o["dma"] is None and not o["bar"]:
                    continue
                s, c = dj["signal"]
                if clk.get(s, 0) < c:
                    waits[s] = max(waits.get(s, 0), c)
                    for k, v in dj["clock"].items():
                        if clk.get(k, 0) < v:
                            clk[k] = v
            o["waits"] = waits
            if o["fn"] is None:
                o["signal"] = None
                o["clock"] = None
            elif o["dma"] is not None:
                s = o["dma"]
                cnt[s] = cnt.get(s, 0) + 16 * (len(o["fn"]) if isinstance(o["fn"], list) else 1)
                o["signal"] = (s, cnt[s])
                ck = dict(clk)
                ck[s] = cnt[s]
                o["clock"] = ck
            elif need[i]:
                s = "E_" + E
                cnt[s] = cnt.get(s, 0) + 1
                o["signal"] = (s, cnt[s])
                ck = dict(clk)
                ck[s] = cnt[s]
                o["clock"] = ck
            else:
                o["signal"] = None
                o["clock"] = None
        return cnt

    def emit(self, nc, es):
        cnt = self.analyze()
        sems = {}
        for s in sorted(cnt):
            sems[s] = es.enter_context(nc.semaphore(s))
        block = es.enter_context(nc.Block())
        ops = self.ops

        def run(engname):
            def body(eng):
                for o in ops:
                    if o["eng"] != engname:
                        continue
                    for s, c in o["waits"].items():
                        eng.wait_ge(sems[s], c)
                    if o["fn"] is None:
                        continue
                    if isinstance(o["fn"], list):
                        for f_ in o["fn"]:
                            f_(eng).then_inc(sems[o["signal"][0]], 16)
                        continue
                    ins = o["fn"](eng)
                    if o["signal"] is not None:
                        s, c = o["signal"]
                        ins.then_inc(sems[s], 16 if o["dma"] is not None else 1)
                if engname == "sp":
                    for s, c in cnt.items():
                        if not s.startswith("E_"):
                            eng.wait_ge(sems[s], c)

            return body

        block.tensor(run("pe"))
        block.scalar(run("act"))
        block.vector(run("dve"))
        block.gpsimd(run("pool"))
        block.sync(run("sp"))
        self.stats = {e: sum(1 for o in ops if o["eng"] == e and o["fn"] is not None) for e in self.ENGS}
        self.stats["waits"] = sum(len(o["waits"]) for o in ops)
        self.stats["sems"] = len(cnt)
        self.stats["maxcnt"] = max(cnt.values())


class Arena:
    def __init__(self, ap):
        self.ap = ap
        self.off = 0

    def alloc(self, free_shape, dtype):
        n = 1
        for s in free_shape:
            n *= s
        nbytes = n * (4 if dtype == F32 else 2)
        nwords = (nbytes + 3) // 4
        nwords = (nwords + 7) // 8 * 8
        v = self.ap[:, self.off:self.off + nwords]
        self.off += nwords
        if dtype != F32:
            v = v.bitcast(dtype)
        v = v[:, 0:n]
        if len(free_shape) == 2:
            v = v.rearrange("p (a b) -> p a b", b=free_shape[1])
        elif len(free_shape) == 3:
            v = v.rearrange("p (a b c) -> p a b c", b=free_shape[1], c=free_shape[2])
        return v


def build(debug=None):
    debug = debug or {}
    nc = bass.Bass("TRN2", target_bir_lowering=False)

    def din(name, shape, dt=F32):
        return nc.dram_tensor(name, list(shape), dt, kind="ExternalInput").ap()

    def dout(name, shape, dt=F32):
        return nc.dram_tensor(name, list(shape), dt, kind="ExternalOutput").ap()

    xp_d = din("xp", [TP, D])
    xm_d = din("xm", [NM, D])
    sh_d = din("sh", [16, 16, 128, 128])
    ss_d = din("ss", [16, 32, 64, 128])
    sc_d = din("sc", [16, 3, 3072])
    cols_d = din("cols", [128, NCOLS])
    w_in = din("w_in", [D, WIN])
    w_bhg = din("w_bhg", [D, D])
    w_bssm = din("w_bssm", [D, D])
    w_out = din("w_out", [D, D])
    w_g = din("w_g", [D, FFN])
    w_u = din("w_u", [D, FFN])
    w_d = din("w_d", [FFN, D])
    nfin_d = din("nfin", [128, D])
    identb_d = din("identb", [128, 128], BF16)
    identf_d = din("identf", [128, 128])
    maskc_d = din("maskc", [128, 128], BF16)
    maskb_d = din("maskb", [128, 128], BF16)
    onesf_d = din("onesf", [128, 128])
    onesb_d = din("onesb", [128, 128], BF16)
    rmp_d = din("rmp", [128, TP], BF16)
    rmm_d = din("rmm", [128, NM], BF16)
    bmask_d = din("bmask", [128, 16], BF16)
    ustr_d = din("ustr", [128, 128])
    tri_d = din("tri", [128, 128])
    eexp_d = din("eexp", [32, 2048])

    y_main = dout("y_main", [TM, D])
    y_samp = dout("y_samp", [TS, D])
    hg_p = dout("hg_p", [16, 128, 128])
    ssm_p = dout("ssm_p", [32, 64, 128])
    conv_p = dout("conv_p", [3, 3072])
    hg_s = dout("hg_s", [16, 16, 128, 128])
    ssm_s = dout("ssm_s", [16, 32, 64, 128])
    conv_s = dout("conv_s", [16, 3, 3072])
    dbg_d = {k: dout("dbg_" + k, v[0], v[1]) for k, v in debug.get("outs", {}).items()}

    P = Prog()
    es = ExitStack()
    with es:
        def sb(name, shape, dt=F32):
            return es.enter_context(nc.sbuf_tensor("s_" + name, list(shape), dt))

        psum = es.enter_context(nc.psum_tensor("psum", [128, 4096], F32))

        def bank(b, n=512, off=0):
            return psum[:, b * 512 + off: b * 512 + off + n]

        def bankb(b):
            return psum[:, b * 512:(b + 1) * 512].bitcast(BF16)

        cols = sb("cols", [128, NCOLS])
        identb = sb("identb", [128, 128], BF16)
        identf = sb("identf", [128, 128])
        maskc = sb("maskc", [128, 128], BF16)
        maskb = sb("maskb", [128, 128], BF16)
        onesf = sb("onesf", [128, 128])
        onesb = sb("onesb", [128, 128], BF16)
        bmask = sb("bmask", [128, 16], BF16)
        ustr = sb("ustr", [128, 128])
        tri = sb("tri", [128, 128])
        convh = sb("convh", [128, 24, 3])
        lbc = sb("lbc", [128, 16])
        omlc = sb("omlc", [128, 16])
        nomlc = sb("nomlc", [128, 16])
        Shg = sb("Shg", [128, 16, 128])
        Sssm = sb("Sssm", [128, 2048])
        NWS = 5
        wpool = sb("wpool", [128, NWS, 16, 128], BF16)
        small = sb("small", [128, 64])
        AR_WORDS = 41900
        arena_t = sb("arena", [128, AR_WORDS])
        AR = Arena(arena_t[:])

        cload = [(cols, cols_d), (identb, identb_d), (identf, identf_d), (maskc, maskc_d), (maskb, maskb_d),
                 (onesf, onesf_d), (bmask, bmask_d), (ustr, ustr_d), (tri, tri_d), (onesb, onesb_d)]
        for i, (t, d_) in enumerate(cload):
            P.dma("sp", "d_c%d" % i, lambda e, t=t, d_=d_: e.dma_start(out=t[:], in_=d_), w=[("c", i)])
        CK = [("c", i) for i in range(len(cload))]
        P.dve(lambda e: e.tensor_tensor(out=lbc[:], in0=cols[:, C_L0:C_L0 + 16], in1=cols[:, C_L1:C_L1 + 16], op=ALU.subtract), r=CK, w=["lbc"])
        P.act(lambda e: e.activation(out=omlc[:], in_=lbc[:], func=AF.Sigmoid, scale=-1.0), r=["lbc"], w=["omlc"])
        P.act(lambda e: e.activation(out=lbc[:], in_=lbc[:], func=AF.Sigmoid), r=["lbc", "omlc"], w=["lbc"])
        P.dve(lambda e: e.tensor_scalar(out=nomlc[:], in0=omlc[:], scalar1=-1.0, scalar2=None, op0=ALU.mult), r=["omlc"], w=["nomlc"])
        P.pool(lambda e: e.memset(Shg[:], 0.0), w=[("Shg", h) for h in range(16)])
        P.pool(lambda e: e.memset(Sssm[:], 0.0), w=[("Sssm", k) for k in range(16)])
        P.pool(lambda e: e.memset(convh[:], 0.0), w=["convh"])
        epsc_t = sb("epsc", [128, 1])
        epsc = epsc_t[:, 0:1]
        P.pool(lambda e: e.memset(epsc_t[:], EPS), w=["epsc"])
        PS6 = [("ps", 6)]

        wstate = {"all": [list(range(NWS)), 0], "hp": [[0, 1, 2], 0], "mp": [[3, 4], 0]}

        def wslot(pool="all"):
            lst, n = wstate[pool]
            wstate[pool][1] = n + 1
            return lst[n % len(lst)]

        def wfetch(src_ap, pool="all"):
            s = wslot(pool)
            srcv = src_ap.rearrange("(c p) n -> p c n", p=128)
            P.dma_multi("pool", "d_w%d" % s,
                        [lambda e, s=s, h_=h_: e.dma_start(out=wpool[:, s, 8 * h_:8 * h_ + 8, :], in_=srcv[:, 8 * h_:8 * h_ + 8, :]) for h_ in range(2)],
                        w=[("w", s)])
            return s

        def norm_T(t, src, src_keys, xnb, xn_keys, junk_ap, junk_keys, hT, hkey, ncol_off, extra_w=()):
            sl = t % 2
            ba, bd = (4, 5) if sl == 0 else (6, 7)
            P.act(lambda e: e.activation(out=junk_ap, in_=src, func=AF.Square, accum_out=ssq[:, sl:sl + 1]), r=src_keys, w=junk_keys + [("ssq", sl)])
            P.act(lambda e: e.activation(out=rstd[:, sl:sl + 1], in_=ssq[:, sl:sl + 1], func=AF.Sqrt, scale=1.0 / D, bias=EPS), r=[("ssq", sl)], w=[("rstd", sl)])
            P.dve(lambda e: e.reciprocal(out=rstd[:, sl:sl + 1], in_=rstd[:, sl:sl + 1]), r=[("rstd", sl)], w=[("rstd", sl)])
            P.dve(lambda e: e.tensor_scalar(out=xnb, in0=src, scalar1=rstd[:, sl:sl + 1], scalar2=None, op0=ALU.mult), r=src_keys + [("rstd", sl)], w=xn_keys)

            def ftr(e):
                for c in range(16):
                    b = ba if c < 8 else bd
                    ins = e.transpose(out=bankb(b)[:, (c % 8) * 128:(c % 8 + 1) * 128], in_=xnb[:, c * 128:(c + 1) * 128], identity=identb[:])
                return ins
            P.pe(ftr, r=xn_keys + CK, w=[("ps", ba), ("ps", bd)])
            for c in range(16):
                b = ba if c < 8 else bd
                src_ps = bankb(b)[:, (c % 8) * 128:(c % 8 + 1) * 128]
                dst = hT[:, c, t * 128:(t + 1) * 128]
                nw = cols[:, ncol_off + c:ncol_off + c + 1]
                if c < 8:
                    P.act(lambda e, src_ps=src_ps, dst=dst, nw=nw: e.mul(out=dst, in_=src_ps, mul=nw), r=[("ps", b)] + CK, w=[(hkey, t)] + list(extra_w))
                else:
                    P.dve(lambda e, src_ps=src_ps, dst=dst, nw=nw: e.tensor_scalar(out=dst, in0=src_ps, scalar1=nw, scalar2=None, op0=ALU.mult), r=[("ps", b)] + CK, w=[(hkey, t)] + list(extra_w))

        def stage_A(xsrc, ntiles, hT, hkey, ncol_off, xt2, xn, ssq, rstd, junk):
            for t in range(ntiles):
                sl = t % 2
                P.dma("sp", "d_x%d" % sl, lambda e, t=t, sl=sl: e.dma_start(out=xt2[:, sl], in_=xsrc[t * 128:(t + 1) * 128, :]), w=[("xt", sl)])
                norm_T(t, xt2[:, sl], [("xt", sl)], (xn, xnB)[sl], [("xn", sl)], junk, ["junk"], hT, hkey, ncol_off)

        def hgrn_head(h, ph, hT, hkey, W, rm, wslots, part="both", bk=None):
            M = ph == "M"
            KY = W["_k"]
            bk = bk or dict(proj=[0, 1, 2], v=3, tr=5, ds=6)
            BV, BT, BD = bk["v"], bk["tr"], bk["ds"]
            EARLY = part in ("early", "both")
            LATE = part in ("late", "both")
            N = NM if M else TP
            nt = 9 if M else 8
            blocks = [(0, 512), (512, 512)] + ([(1024, 128)] if M else [])
            sf, sv = wslots["f"], wslots["v"]
            hk = [(hkey, t) for t in range(nt)]
            lb = lbc[:, h:h + 1]
            oml = omlc[:, h:h + 1]
            noml = nomlc[:, h:h + 1]

            def proj_fm(slot, blk, bi):
                c0, n = blk

                def f(e):
                    for c in range(16):
                        ins = e.matmul(bank(bi, n), lhsT=wpool[:, slot, c, :], rhs=hT[:, c, c0:c0 + n], start=(c == 0), stop=(c == 15))
                    return ins
                P.pe(f, r=[("w", slot)] + [(hkey, t) for t in range(c0 // 128, (c0 + n) // 128)], w=[("ps", bi)])

            blast = W["blast"]
            elast = W["elast"]
            lf = W["lf"]
            bb = W["bb"]
            bbm = bb[:, 0:1024].rearrange("p (t k) -> p t k", k=128)
            bbs = bb[:, 1024:1152].rearrange("p (s k) -> p s k", k=8)
            if EARLY:
                sq, sg = wslots.get("q"), wslots.get("g")

                def secA():
                    for i, blk in enumerate(blocks):
                        bi = bk["proj"][i % len(bk["proj"])]
                        proj_fm(sf, blk, bi)
                        P.act(lambda e, blk=blk, bi=bi: e.activation(out=W["T1"][:, blk[0]:blk[0] + blk[1]], in_=bank(bi, blk[1]), func=AF.Sigmoid), r=[("ps", bi)], w=[KY["T1"]])

                def secB():
                    P.act(lambda e: e.activation(out=W["lf"][:, 0:N], in_=W["T1"][:, 0:N], func=AF.Ln, bias=lb, scale=oml), r=[KY["T1"], "lbc", "omlc"], w=[KY["lf"]])
                    P.dve(lambda e: e.tensor_scalar(out=W["kkb"][:, 0:N], in0=W["T1"][:, 0:N], scalar1=noml, scalar2=oml, op0=ALU.mult, op1=ALU.add), r=[KY["T1"], "nomlc", "omlc"], w=[KY["kkb"]])
                    P.dve(lambda e: e.tensor_tensor_scan(out=W["bb"][:, 0:N], data0=rm[:, 0:N], data1=W["lf"][:, 0:N], initial=0.0, op0=ALU.mult, op1=ALU.add), r=[KY["lf"], "rm"], w=[KY["bb"]])
                    bb = W["bb"]
                    bbm = bb[:, 0:1024].rearrange("p (t k) -> p t k", k=128)
                    blast = W["blast"]
                    elast = W["elast"]
                    P.dve(lambda e: e.tensor_copy(out=blast[:, 0:8], in_=bbm[:, :, 127]), r=[KY["bb"]], w=[KY["blast"]])
                    if M:
                        bbs = bb[:, 1024:1152].rearrange("p (s k) -> p s k", k=8)
                        P.dve(lambda e: e.tensor_copy(out=blast[:, 8:24], in_=bbs[:, :, 7]), r=[KY["bb"]], w=[KY["blast"]])
                        P.dve(lambda e: e.tensor_copy(out=W["bref"][:, 0:8], in_=bbm[:, :, 63]), r=[KY["bb"]], w=[KY["bref"]])
                    nb = 24 if M else 8
                    P.act(lambda e: e.activation(out=elast[:, 0:nb], in_=blast[:, 0:nb], func=AF.Exp), r=[KY["blast"]], w=[KY["elast"]])
                    lf = W["lf"]
                    P.dve(lambda e: e.tensor_tensor(out=lf[:, 0:1024].rearrange("p (t k) -> p t k", k=128), in0=blast[:, 0:8].unsqueeze(2).to_broadcast([128, 8, 128]), in1=bbm, op=ALU.subtract), r=[KY["bb"], KY["blast"]], w=[KY["lf"]])
                    if M:
                        P.dve(lambda e: e.tensor_tensor(out=lf[:, 1024:1152].rearrange("p (s k) -> p s k", k=8), in0=blast[:, 8:24].unsqueeze(2).to_broadcast([128, 16, 8]), in1=bbs, op=ALU.subtract), r=[KY["bb"], KY["blast"]], w=[KY["lf"]])
                    P.act(lambda e: e.activation(out=W["T2"][:, 0:N], in_=lf[:, 0:N], func=AF.Exp), r=[KY["lf"]], w=[KY["T2"]])
                    P.dve(lambda e: e.tensor_tensor(out=W["kl"][:, 0:N], in0=W["kkb"][:, 0:N], in1=W["T2"][:, 0:N], op=ALU.mult), r=[KY["kkb"], KY["T2"]], w=[KY["kl"]])

                def secC():
                    for i, blk in enumerate(blocks):
                        bi = bk["proj"][i % len(bk["proj"])]
                        proj_fm(sq, blk, bi)
                        P.act(lambda e, blk=blk, bi=bi: e.activation(out=W["qb"][:, blk[0]:blk[0] + blk[1]], in_=bank(bi, blk[1]), func=AF.Copy, scale=float(128 ** -0.5)), r=[("ps", bi)], w=[KY["qb"]])

                def secD():
                    for i, blk in enumerate(blocks):
                        bi = bk["proj"][i % len(bk["proj"])]
                        proj_fm(sg, blk, bi)
                        P.act(lambda e, blk=blk, bi=bi: e.activation(out=W["gs"][:, blk[0]:blk[0] + blk[1]], in_=bank(bi, blk[1]), func=AF.Silu), r=[("ps", bi)], w=[KY["gs"]])

                def secE():
                    P.dve(lambda e: e.tensor_tensor(out=lf[:, 0:N].rearrange("p (t k) -> p t k", k=128), in0=bb[:, 0:N].rearrange("p (t k) -> p t k", k=128), in1=W["bref"][:, 0:9].unsqueeze(2).to_broadcast([128, 9, 128]), op=ALU.subtract), r=[KY["bb"], KY["bref"], KY["T2"]], w=[KY["lf"]])
                    P.act(lambda e: e.activation(out=W["T1"][:, 0:N], in_=lf[:, 0:N], func=AF.Exp), r=[KY["lf"], KY["kkb"]], w=[KY["T1"]])
                    P.dve(lambda e: e.tensor_tensor(out=W["qm"][:, 0:N], in0=W["qb"][:, 0:N], in1=W["T1"][:, 0:N], op=ALU.mult), r=[KY["qb"], KY["T1"]], w=[KY["qm"]])
                    P.act(lambda e: e.activation(out=W["T2"][:, 0:N], in_=lf[:, 0:N], func=AF.Exp, scale=-1.0), r=[KY["lf"], KY["kl"]], w=[KY["T2"]])
                    P.dve(lambda e: e.tensor_tensor(out=W["km"][:, 0:N], in0=W["kkb"][:, 0:N], in1=W["T2"][:, 0:N], op=ALU.mult), r=[KY["kkb"], KY["T2"]], w=[KY["km"]])
                    P.act(lambda e: e.activation(out=W["T1"][:, 0:N], in_=bb[:, 0:N], func=AF.Exp), r=[KY["bb"], KY["qm"]], w=[KY["T1"]])
                    P.dve(lambda e: e.tensor_tensor(out=W["qn"][:, 0:N], in0=W["qb"][:, 0:N], in1=W["T1"][:, 0:N], op=ALU.mult), r=[KY["qb"], KY["T1"]], w=[KY["qn"]])

                def secF():
                    vT = W["vT"]
                    for i, blk in enumerate(blocks):
                        bi = bk["proj"][i % len(bk["proj"])]
                        proj_fm(sv, blk, bi)
                        P.act(lambda e, blk=blk, bi=bi: e.copy(out=vT[:, blk[0]:blk[0] + blk[1]], in_=bank(bi, blk[1])), r=[("ps", bi)], w=[KY["vT"]])
                    for r0 in range(0, nt, 8):
                        tiles = list(range(r0, min(nt, r0 + 8)))

                        def fvt(e, tiles=tiles):
                            for j, t in enumerate(tiles):
                                ins = e.transpose(out=bankb(BV)[:, j * 128:(j + 1) * 128], in_=vT[:, t * 128:(t + 1) * 128], identity=identb[:])
                            return ins
                        P.pe(fvt, r=[KY["vT"]] + CK, w=[("ps", BV)])
                        n = len(tiles) * 128
                        P.act(lambda e, r0=r0, n=n: e.copy(out=W["vtok"][:, r0:r0 + n // 128, :], in_=bankb(BV)[:, 0:n].rearrange("p (t k) -> p t k", k=128)), r=[("ps", BV)], w=[KY["vtok"]])

                secA()
                if M:
                    P.interleave(P.capture(lambda: (secC(), secD())), P.capture(secB))
                    P.interleave(P.capture(secF), P.capture(secE))
                else:
                    P.interleave(P.capture(secF), P.capture(secB))
            if not LATE:
                return
            for r0 in range(0, nt, 4):
                tiles = list(range(r0, min(nt, r0 + 4)))
                n = len(tiles) * 128
                samp = (tiles[0] == 8)
                def ftr(e, tiles=tiles):
                    for j, t in enumerate(tiles):
                        ins = e.transpose(out=bankb(BT)[:, j * 128:(j + 1) * 128], in_=W["kl"][:, t * 128:(t + 1) * 128], identity=identb[:])
                    return ins
                P.pe(ftr, r=[KY["kl"]] + CK, w=[("ps", BT)])
                P.act(lambda e, r0=r0, n=n: e.copy(out=W["ktok"][:, r0:r0 + n // 128, :], in_=bankb(BT)[:, 0:n].rearrange("p (t k) -> p t k", k=128)), r=[("ps", BT)], w=[KY["ktok"]])
                if M:
                    def fsc(e, tiles=tiles):
                        for j, t in enumerate(tiles):
                            e.matmul(psum[0:64, 4 * 512 + j * 128:4 * 512 + (j + 1) * 128], lhsT=W["km"][:, t * 128:t * 128 + 64], rhs=W["qm"][:, t * 128:(t + 1) * 128], start=True, stop=True)
                            ins = e.matmul(psum[64:128, 4 * 512 + j * 128 + 64:4 * 512 + (j + 1) * 128], lhsT=W["km"][:, t * 128 + 64:(t + 1) * 128], rhs=W["qm"][:, t * 128 + 64:(t + 1) * 128], start=True, stop=True)
                        return ins
                    P.pe(fsc, r=[KY["km"], KY["qm"]], w=[("ps", 4)])
                    mk = maskb if samp else maskc
                    nt_ = n // 128
                    P.dve(lambda e, r0=r0, nt_=nt_, mk=mk: e.tensor_tensor(out=W["scm"][0:64, r0:r0 + nt_, :], in0=psum[0:64, 4 * 512:4 * 512 + nt_ * 128].rearrange("p (t k) -> p t k", k=128), in1=mk[0:64, :].unsqueeze(1).to_broadcast([64, nt_, 128]), op=ALU.mult), r=[("ps", 4)] + CK, w=[KY["scm"]])
                    P.dve(lambda e, r0=r0, nt_=nt_, mk=mk: e.tensor_tensor(out=W["scm"][64:128, r0:r0 + nt_, 64:128], in0=psum[64:128, 4 * 512:4 * 512 + nt_ * 128].rearrange("p (t k) -> p t k", k=128)[:, :, 64:128], in1=mk[64:128, 64:128].unsqueeze(1).to_broadcast([64, nt_, 64]), op=ALU.mult), r=[("ps", 4)] + CK, w=[KY["scm"]])
                if not samp:
                    def fds(e, tiles=tiles):
                        for j, t in enumerate(tiles):
                            ins = e.matmul(bank(BD, 128, j * 128), lhsT=W["ktok"][:, t, :], rhs=W["vtok"][:, t, :], start=True, stop=True)
                        return ins
                    P.pe(fds, r=[KY["ktok"], KY["vtok"]], w=[("ps", BD)])
                    for j, t in enumerate(tiles):
                        if M:
                            P.act(lambda e, t=t: e.copy(out=W["Sbf"][:, t, :], in_=Shg[:, h, :]), r=[("Shg", h)], w=[("Sbf", t)])
                        P.dve(lambda e, j=j, t=t: e.scalar_tensor_tensor(out=Shg[:, h, :], in0=Shg[:, h, :], scalar=elast[:, t:t + 1], in1=bank(BD, 128, j * 128), op0=ALU.mult, op1=ALU.add), r=[("Shg", h), KY["elast"], ("ps", BD)], w=[("Shg", h)])
                    if M and tiles[-1] == 7:
                        P.dma("sp", "d_hgp", lambda e: e.dma_start(out=hg_p[h], in_=Shg[:, h, :]), r=[("Shg", h)])
                else:
                    S0 = W["S0"]
                    S0b = W["S0b"]
                    Vb = W["Vb"]
                    P.dma("sp", "d_s0", lambda e: e.dma_start(out=S0, in_=sh_d[:, h].rearrange("s d v -> d s v")), r=[], w=[KY["S0"]])
                    P.act(lambda e: e.copy(out=S0b, in_=S0), r=[KY["S0"]], w=[KY["S0b"]])
                    P.dve(lambda e: e.tensor_tensor(out=Vb, in0=W["vtok"][:, 8, :].unsqueeze(1).to_broadcast([128, 16, 128]), in1=bmask[:].unsqueeze(2).to_broadcast([128, 16, 128]), op=ALU.mult), r=[KY["vtok"]] + CK, w=[KY["Vb"]])

                    P.dve(lambda e: e.tensor_tensor(out=S0, in0=S0, in1=elast[:, 8:24].unsqueeze(2).to_broadcast([128, 16, 128]), op=ALU.mult), r=[KY["S0"], KY["elast"], KY["S0b"]], w=[KY["S0"]])
                    for half in range(2):
                        def fbl(e, half=half):
                            for u in range(2):
                                ins = e.matmul(bank(4 + u, 512), lhsT=W["ktok"][:, 8, :], rhs=Vb[:, 8 * half + 4 * u:8 * half + 4 * u + 4, :], start=True, stop=True)
                            return ins
                        P.pe(fbl, r=[KY["ktok"], KY["Vb"]], w=[("ps", 4), ("ps", 5)])
                        P.dve(lambda e, half=half: e.tensor_tensor(out=S0[:, 8 * half:8 * half + 8, :], in0=S0[:, 8 * half:8 * half + 8, :], in1=psum[:, 2048:3072].rearrange("p (s v) -> p s v", v=128), op=ALU.add), r=[KY["S0"], ("ps", 4), ("ps", 5)], w=[KY["S0"]])
                    P.dma("sp", "d_hgs", lambda e: e.dma_start(out=hg_s[:, h].rearrange("s d v -> d s v"), in_=S0), r=[KY["S0"]])
                if M:
                    def fo(e, tiles=tiles, samp=samp):
                        for j, t in enumerate(tiles):
                            o_ = bank(7, 128, j * 128)
                            ins = e.matmul(o_, lhsT=W["vtok"][:, t, :], rhs=W["scm"][:, t, :], start=True, stop=samp)
                            if not samp:
                                ins = e.matmul(o_, lhsT=W["Sbf"][:, t, :], rhs=W["qn"][:, t * 128:(t + 1) * 128], start=False, stop=True)
                        return ins
                    P.pe(fo, r=[KY["vtok"], KY["scm"], KY["qn"]] + [("Sbf", t) for t in tiles], w=[("ps", 7)])
                    if samp:
                        def fos(e):
                            for s_ in range(16):
                                ins = e.matmul(bank(5, 8, 8 * s_), lhsT=W["S0b"][:, s_, :], rhs=W["qn"][:, 1024 + 8 * s_:1024 + 8 * s_ + 8], start=True, stop=True)
                            return ins
                        P.pe(fos, r=[KY["qn"], KY["S0b"]], w=[("ps", 5)])
                        P.act(lambda e: e.copy(out=W["on"][:, 0:128], in_=bank(5, 128)), r=[("ps", 5)], w=[KY["on"]])
                        P.dve(lambda e: e.tensor_tensor(out=W["on"][:, 0:128], in0=W["on"][:, 0:128], in1=bank(7, 128), op=ALU.add), r=[KY["on"], ("ps", 7)], w=[KY["on"]])
                    c0 = r0 * 128
                    osq, rs, on = W["osq"], W["rs"], W["on"]
                    osrc = on[:, 0:n] if samp else bank(7, n)
                    okey = KY["on"] if samp else ("ps", 7)
                    P.act(lambda e, n=n, osrc=osrc: e.activation(out=osq[:, 0:n], in_=osrc, func=AF.Square), r=[okey], w=[KY["osq"]])
                    P.pe(lambda e, n=n: e.matmul(bank(6, n), lhsT=onesb[:], rhs=osq[:, 0:n], start=True, stop=True), r=[KY["osq"]] + CK, w=PS6)
                    P.act(lambda e, n=n: e.activation(out=rs[:, 0:n], in_=bank(6, n), func=AF.Ln, scale=1.0 / 128, bias=epsc), r=PS6, w=[KY["rs"]])
                    P.act(lambda e, n=n: e.activation(out=rs[:, 0:n], in_=rs[:, 0:n], func=AF.Exp, scale=-0.5), r=[KY["rs"]], w=[KY["rs"]])
                    P.dve(lambda e, n=n, osrc=osrc: e.tensor_tensor(out=on[:, 0:n], in0=osrc, in1=rs[:, 0:n], op=ALU.mult), r=[okey, KY["rs"]], w=[KY["on"]])
                    P.dve(lambda e, n=n, c0=c0: e.scalar_tensor_tensor(out=oT[:, h, c0:c0 + n], in0=on[:, 0:n], scalar=cols[:, C_HGN + h:C_HGN + h + 1], in1=W["gs"][:, c0:c0 + n], op0=ALU.mult, op1=ALU.mult), r=[KY["on"], KY["gs"]] + CK, w=[("oT", h)])

        def carve_mamba(base=None, limit=None):
            full = base is None
            AR.off = work_off if base is None else base
            Wm = {}
            for k_ in ("acumT", "R1", "R2"):
                Wm[k_] = AR.alloc([NM], F32)
            for k_ in ("dt_tok", "dA_tok", "w_tok"):
                Wm[k_] = AR.alloc([288], F32)
            Wm["eab"] = AR.alloc([256], F32)
            Wm["alastT"] = AR.alloc([24], F32)
            for k_ in ("BcT", "CcT", "xcT"):
                Wm[k_] = AR.alloc([NM], BF16)
            for k_ in ("Btok", "xdtd"):
                Wm[k_] = AR.alloc([9, 128], BF16)
            dummy = Wm["R1"]
            if not full:
                Wm["xcT2"] = AR.alloc([NM], BF16)
            if full:
                Wm["raws"] = AR.alloc([176], F32)
                for k_ in ("cbm", "xdt"):
                    Wm[k_] = AR.alloc([9, 128], BF16)
                Wm["Bblk"] = AR.alloc([4, 128], BF16)
                Wm["S0Tb"] = AR.alloc([4, 128], BF16)
                Wm["eax"] = AR.alloc([512], F32)
                Wm["zs"] = AR.alloc([NM], BF16)
                Wm["Rex"] = AR.alloc([2, 256], F32)
                Wm["Lb"] = AR.alloc([2, 256], BF16)
                Wm["Mt"] = AR.alloc([2, 2, 128], BF16)
                Wm["t1"] = AR.alloc([512], F32)
                Wm["t2"] = AR.alloc([512], F32)
                Wm["sq"] = AR.alloc([512], BF16)
                Wm["STb"] = AR.alloc([2, 128], BF16)
                Wm["stg"] = AR.alloc([64], F32)
                Wm["stgT"] = AR.alloc([128], F32)
                Wm["eexpk"] = AR.alloc([128], F32)
                Wm["eaxl"] = AR.alloc([16], F32)
                Wm["t1s"] = AR.alloc([128], F32)
                Wm["t2s"] = AR.alloc([128], F32)
            else:
                for k_ in ("raws", "cbm", "xdt", "Bblk", "S0Tb", "eax", "zs", "Rex", "Lb", "Mt", "t1", "t2", "sq", "STb", "stg", "stgT", "eexpk", "eaxl", "t1s", "t2s"):
                    Wm[k_] = dummy
            assert AR.off <= (AR_WORDS if limit is None else limit), AR.off
            return Wm

        def rounds(nt):
            return [list(range(r0, min(nt, r0 + 4))) for r0 in range(0, nt, 4)]

        def mamba(ph, hT, hkey, rm, dual=False, pre_dt=None):
            M = ph == "M"
            wp = "mp" if dual else "all"
            BMAP = {0: 4, 1: 5, 2: 6, 3: 6, 6: 7} if dual else {}

            def mbk(b):
                return BMAP.get(b, b)
            N = NM if M else TP
            nt = 9 if M else 8
            blocks = [(0, 512), (512, 512)] + ([(1024, 128)] if M else [])
            Wm = carve_mamba(9216, 18432) if dual else carve_mamba()
            acumT, R1, R2 = Wm["acumT"], Wm["R1"], Wm["R2"]
            dt_tok, dA_tok, w_tok, eab, alastT = Wm["dt_tok"], Wm["dA_tok"], Wm["w_tok"], Wm["eab"], Wm["alastT"]
            BcT, CcT, xcT, Btok, cbm, xdt, xdtd = Wm["BcT"], Wm["CcT"], Wm["xcT"], Wm["Btok"], Wm["cbm"], Wm["xdt"], Wm["xdtd"]
            rawm = R1[:, 0:1027]
            raws = Wm["raws"]
            raws3 = raws[:, 0:176].rearrange("p (s k) -> p s k", k=11)
            ssqacc = R2
            S0h = R1[:, 0:1024].rearrange("p (s n) -> p s n", n=128)

            def hk(blk):
                return [(hkey, t) for t in range(blk[0] // 128, (blk[0] + blk[1]) // 128)]

            if pre_dt is not None:
                s_dt = pre_dt
            else:
                s_dt = wslot(wp)
                P.dma("pool", "d_w%d" % s_dt, lambda e: e.dma_start(out=wpool[:, s_dt, :, 0:32], in_=w_in[:, ODT:ODT + 32].rearrange("(c p) n -> p c n", p=128)), w=[("w", s_dt)])
            for i, blk in enumerate(blocks):
                bi = mbk(i % 2)
                c0, n = blk

                def f(e, bi=bi, c0=c0, n=n):
                    for c in range(16):
                        ins = e.matmul(psum[0:32, bi * 512:bi * 512 + n], lhsT=wpool[:, s_dt, c, 0:32], rhs=hT[:, c, c0:c0 + n], start=(c == 0), stop=(c == 15))
                    return ins
                P.pe(f, r=[("w", s_dt)] + hk(blk), w=[("ps", bi)])
                P.act(lambda e, bi=bi, c0=c0, n=n: e.activation(out=R1[0:32, c0:c0 + n], in_=psum[0:32, bi * 512:bi * 512 + n], func=AF.Exp, bias=cols[0:32, C_DTB:C_DTB + 1]), r=[("ps", bi)] + CK, w=["R1"])
            P.act(lambda e: e.activation(out=R1[0:32, 0:N], in_=R1[0:32, 0:N], func=AF.Ln, bias=1.0), r=["R1"], w=["R1"])
            Acol = small[0:32, 8:9]
            P.act(lambda e: e.activation(out=Acol, in_=cols[0:32, C_ALOG:C_ALOG + 1], func=AF.Exp), r=CK, w=["Acol"])
            P.dve(lambda e: e.tensor_scalar(out=Acol, in0=Acol, scalar1=-1.0, scalar2=None, op0=ALU.mult), r=["Acol"], w=["Acol"])
            P.dve(lambda e: e.tensor_scalar(out=R2[0:32, 0:N], in0=R1[0:32, 0:N], scalar1=Acol, scalar2=None, op0=ALU.mult), r=["R1", "Acol"], w=["R2"])
            P.dve(lambda e: e.tensor_tensor_scan(out=acumT[0:32, 0:N], data0=rm[0:32, 0:N], data1=R2[0:32, 0:N], initial=0.0, op0=ALU.mult, op1=ALU.add), r=["R2", "rm"], w=["acumT"])

            def tr32(src, dst, ks, kd):
                def f(e):
                    for t in range(nt):
                        ins = e.transpose(out=bank(mbk(2), 32, t * 32), in_=src[0:32, t * 128:(t + 1) * 128], identity=identf[0:32, 0:32])
                    return ins
                P.pe(f, r=[ks] + CK, w=[("ps", mbk(2))])
                P.act(lambda e: e.copy(out=dst[:, 0:nt * 32], in_=bank(mbk(2), nt * 32)), r=[("ps", mbk(2))], w=[kd])
            tr32(R1, dt_tok, "R1", "dt_tok")
            tr32(R2, dA_tok, "R2", "dA_tok")
            acm = acumT[0:32, 0:1024].rearrange("p (t k) -> p t k", k=128)
            P.dve(lambda e: e.tensor_copy(out=alastT[0:32, 0:8], in_=acm[:, :, 127]), r=["acumT"], w=["alastT"])
            P.dve(lambda e: e.tensor_tensor(out=R2[0:32, 0:1024].rearrange("p (t k) -> p t k", k=128), in0=alastT[0:32, 0:8].unsqueeze(2).to_broadcast([32, 8, 128]), in1=acm, op=ALU.subtract), r=["acumT", "alastT"], w=["R2"])
            if M:
                acs = acumT[0:32, 1024:1152].rearrange("p (s k) -> p s k", k=8)
                P.dve(lambda e: e.tensor_copy(out=alastT[0:32, 8:24], in_=acs[:, :, 7]), r=["acumT"], w=["alastT"])
                P.dve(lambda e: e.tensor_tensor(out=R2[0:32, 1024:1152].rearrange("p (s k) -> p s k", k=8), in0=alastT[0:32, 8:24].unsqueeze(2).to_broadcast([32, 16, 8]), in1=acs, op=ALU.subtract), r=["acumT", "alastT"], w=["R2"])
            P.act(lambda e: e.activation(out=R2[0:32, 0:N], in_=R2[0:32, 0:N], func=AF.Exp), r=["R2"], w=["R2"])
            P.dve(lambda e: e.tensor_tensor(out=R2[0:32, 0:N], in0=R2[0:32, 0:N], in1=R1[0:32, 0:N], op=ALU.mult), r=["R2", "R1"], w=["R2"])
            tr32(R2, w_tok, "R2", "w_tok")
            P.pe(lambda e: e.matmul(bank(mbk(3), 256), lhsT=onesf[:], rhs=dA_tok[:, 0:256], start=True, stop=True), r=["dA_tok"] + CK, w=[("ps", mbk(3))])
            P.act(lambda e: e.activation(out=eab[:, 0:256], in_=bank(mbk(3), 256), func=AF.Exp), r=[("ps", mbk(3))], w=["eab"])

            def proj_conv(cb, slot, dst, dkey):
                cwc = [cols[:, C_CW + j * 24 + cb:C_CW + j * 24 + cb + 1] for j in range(4)]
                cbc = cols[:, C_CB + cb:C_CB + cb + 1]
                if M:
                    P.dve(lambda e: e.tensor_copy(out=rawm[:, 0:3], in_=convh[:, cb, :]), r=["convh"], w=["R1"])
                    P.dma("sp", "d_sc", lambda e: e.dma_start(out=Wm["stgT"][0:48, :], in_=sc_d.rearrange("s j c -> (s j) c")[:, cb * 128:(cb + 1) * 128]), w=["stgT"])
                    P.pe(lambda e: e.transpose(out=bank(1, 48), in_=Wm["stgT"][0:48, :], identity=identf[0:48, 0:48]), r=["stgT"] + CK, w=[("ps", 1)])
                    P.act(lambda e: e.copy(out=raws3[:, :, 0:3], in_=bank(1, 48).rearrange("p (s k) -> p s k", k=3)), r=[("ps", 1)], w=["raws"])
                else:
                    P.pool(lambda e: e.memset(rawm[:, 0:3], 0.0), w=["R1"])
                for i, blk in enumerate(blocks):
                    bi = mbk(i % 2)
                    c0, n = blk

                    def f(e, bi=bi, c0=c0, n=n):
                        for c in range(16):
                            ins = e.matmul(bank(bi, n), lhsT=wpool[:, slot, c, :], rhs=hT[:, c, c0:c0 + n], start=(c == 0), stop=(c == 15))
                        return ins
                    P.pe(f, r=[("w", slot)] + hk(blk), w=[("ps", bi)])
                    if c0 < 1024:
                        P.act(lambda e, bi=bi, c0=c0, n=n: e.copy(out=rawm[:, 3 + c0:3 + c0 + n], in_=bank(bi, n)), r=[("ps", bi)], w=["R1"])
                    else:
                        P.act(lambda e, bi=bi: e.copy(out=raws3[:, :, 3:11], in_=bank(bi, 128).rearrange("p (s k) -> p s k", k=8)), r=[("ps", bi)], w=["raws"])
                for ci_, (c0, n) in enumerate([(0, 512), (512, 512)]):
                    ci = mbk(ci_)
                    acc = bank(ci, n)
                    P.dve(lambda e, acc=acc, c0=c0, n=n: e.tensor_scalar(out=acc, in0=rawm[:, c0:c0 + n], scalar1=cwc[0], scalar2=cbc, op0=ALU.mult, op1=ALU.add), r=["R1"] + CK, w=[("ps", ci)])
                    for j in range(1, 4):
                        P.dve(lambda e, acc=acc, c0=c0, n=n, j=j: e.scalar_tensor_tensor(out=acc, in0=rawm[:, c0 + j:c0 + j + n], scalar=cwc[j], in1=acc, op0=ALU.mult, op1=ALU.add), r=["R1", ("ps", ci)] + CK, w=[("ps", ci)])
                    P.act(lambda e, acc=acc, c0=c0, n=n: e.activation(out=dst[:, c0:c0 + n], in_=acc, func=AF.Silu), r=[("ps", ci)], w=[dkey])
                if M:
                    accs = bank(0, 176)
                    P.dve(lambda e: e.tensor_scalar(out=accs[:, 0:173], in0=raws[:, 0:173], scalar1=cwc[0], scalar2=cbc, op0=ALU.mult, op1=ALU.add), r=["raws"] + CK, w=[("ps", 0)])
                    for j in range(1, 4):
                        P.dve(lambda e, j=j: e.scalar_tensor_tensor(out=accs[:, 0:173], in0=raws[:, j:j + 173], scalar=cwc[j], in1=accs[:, 0:173], op0=ALU.mult, op1=ALU.add), r=["raws", ("ps", 0)] + CK, w=[("ps", 0)])
                    P.act(lambda e: e.activation(out=dst[:, 1024:1152].rearrange("p (s k) -> p s k", k=8), in_=accs.rearrange("p (s k) -> p s k", k=11)[:, :, 0:8], func=AF.Silu), r=[("ps", 0)], w=[dkey])
                    stg, stgT = Wm["stg"], Wm["stgT"]
                    P.dve(lambda e: e.tensor_copy(out=stg[:, 0:48].rearrange("p (s k) -> p s k", k=3), in_=raws3[:, :, 8:11]), r=["raws"], w=["stg"])
                    P.dve(lambda e: e.tensor_copy(out=stg[:, 48:51], in_=rawm[:, 1024:1027]), r=["R1"], w=["stg"])
                    P.pe(lambda e: e.transpose(out=psum[0:51, 512:512 + 128], in_=stg[:, 0:51], identity=identf[:]), r=["stg"] + CK, w=[("ps", 1)])
                    P.act(lambda e: e.copy(out=stgT[0:51, :], in_=psum[0:51, 512:512 + 128]), r=[("ps", 1)], w=["stgT"])
                    P.dma("sp", "d_cvo", lambda e: e.dma_start(out=conv_s.rearrange("s j c -> (s j) c")[:, cb * 128:(cb + 1) * 128], in_=stgT[0:48, :]), r=["stgT"])
                    P.dma("sp", "d_cvo", lambda e: e.dma_start(out=conv_p[:, cb * 128:(cb + 1) * 128], in_=stgT[48:51, :]), r=["stgT"])
                else:
                    P.dve(lambda e: e.tensor_copy(out=convh[:, cb, :], in_=rawm[:, 1024:1027]), r=["R1"], w=["convh"])

            shf = Shg[:].rearrange("p a b -> p (a b)")
            H1 = {"xcT": shf[:, 0:576].bitcast(BF16), "zs": shf[:, 576:1152].bitcast(BF16), "eexpk": shf[:, 1152:1280], "eaxl": shf[:, 1280:1296],
                  "Xq": shf[:, 1296:1808].rearrange("p (s n) -> p s n", n=128), "stgF": shf[:, 1808:1936]}

            def make_group(g):
                def GL_B():
                    sB = wfetch(w_in[:, OB + g * 128:OB + (g + 1) * 128], wp)
                    proj_conv(16 + g, sB, BcT, "BcT")

                def GL_rest():
                    sC = wfetch(w_in[:, OC + g * 128:OC + (g + 1) * 128], wp)
                    proj_conv(20 + g, sC, CcT, "CcT")
                    for tiles in rounds(nt):
                        r0, n_ = tiles[0], len(tiles)

                        def ftb(e, tiles=tiles):
                            for j, t in enumerate(tiles):
                                ins = e.transpose(out=bankb(mbk(2))[:, j * 128:(j + 1) * 128], in_=BcT[:, t * 128:(t + 1) * 128], identity=identb[:])
                            return ins
                        P.pe(ftb, r=["BcT"] + CK, w=[("ps", mbk(2))])
                        P.act(lambda e, r0=r0, n_=n_: e.copy(out=Btok[:, r0:r0 + n_, :], in_=bankb(mbk(2))[:, 0:n_ * 128].rearrange("p (t k) -> p t k", k=128)), r=[("ps", mbk(2))], w=["Btok"])
                        if M:
                            def fcb(e, tiles=tiles):
                                for j, t in enumerate(tiles):
                                    ins = e.matmul(bank(3, 128, j * 128), lhsT=BcT[:, t * 128:(t + 1) * 128], rhs=CcT[:, t * 128:(t + 1) * 128], start=True, stop=True)
                                return ins
                            P.pe(fcb, r=["BcT", "CcT"], w=[("ps", 3)])
                            mk = maskb if r0 == 8 else maskc
                            P.dve(lambda e, r0=r0, n_=n_, mk=mk: e.tensor_tensor(out=cbm[:, r0:r0 + n_, :], in0=bank(3, n_ * 128).rearrange("p (t k) -> p t k", k=128), in1=mk[:].unsqueeze(1).to_broadcast([128, n_, 128]), op=ALU.mult), r=[("ps", 3)] + CK, w=["cbm"])

                def blk_vars(kk, hs):
                    k = g * 4 + kk
                    if hs == 0:
                        return k, xcT, Wm["zs"], Wm["eexpk"], Wm["eaxl"], ("xcT", 0), ("zs", 0), ("eexpk", 0), ("eaxl", 0)
                    if not M:
                        return k, Wm["xcT2"], Wm["zs"], Wm["eexpk"], Wm["eaxl"], ("xcT", 1), ("zs", 0), ("eexpk", 0), ("eaxl", 0)
                    return k, H1["xcT"], H1["zs"], H1["eexpk"], H1["eaxl"], ("xcT", 1), ("zs", 1), ("eexpk", 1), ("eaxl", 1)

                def blk_early(kk, hs):
                    k, xcT_, zs, eexpk, eaxl_, KX, KZ, KE, KA = blk_vars(kk, hs)
                    sx = wfetch(w_in[:, OX + k * 128:OX + (k + 1) * 128], wp)
                    if M:
                        sz = wfetch(w_in[:, OZ + k * 128:OZ + (k + 1) * 128], wp)
                    proj_conv(k, sx, xcT_, KX)
                    if M:
                        P.dma("sp", "d_ee%d" % hs, lambda e: e.dma_start(out=eexpk[0:32, :], in_=eexp_d[:, k * 128:(k + 1) * 128]), w=[KE])
                        for i, (c0, n) in enumerate(blocks):
                            bz = (0, 1)[i % 2]

                            def fz(e, c0=c0, n=n, bz=bz):
                                for c in range(16):
                                    ins = e.matmul(bank(bz, n), lhsT=wpool[:, sz, c, :], rhs=hT[:, c, c0:c0 + n], start=(c == 0), stop=(c == 15))
                                return ins
                            P.pe(fz, r=[("w", sz)] + hk((c0, n)), w=[("ps", bz)])
                            P.act(lambda e, c0=c0, n=n, bz=bz: e.activation(out=zs[:, c0:c0 + n], in_=bank(bz, n), func=AF.Silu), r=[("ps", bz)], w=[KZ])
                        P.pe(lambda e: e.matmul(bank(1, 16), lhsT=eexpk[0:32, :], rhs=alastT[0:32, 8:24], start=True, stop=True), r=[KE, "alastT"], w=[("ps", 1)])
                        P.act(lambda e: e.activation(out=eaxl_[:, 0:16], in_=bank(1, 16), func=AF.Exp), r=[("ps", 1)], w=[KA])


                def blk_late(kk, hs):
                    k, xcT_, zs, eexpk, eaxl_, KX, KZ, KE, KA = blk_vars(kk, hs)
                    SK = ("Sssm", k)
                    Sblk = Sssm[:, k * 128:(k + 1) * 128]
                    def s0q(qd):
                        if qd in (0, 3):
                            return H1["Xq"], "Xq", ["Xq"]
                        return (Wm["t1"], Wm["t2"])[qd - 1].rearrange("p (s n) -> p s n", n=128), ("t1", "t2")[qd - 1], [("t1", "t2")[qd - 1]]

                    def s0_load(qd):
                        S0q, QK, QW = s0q(qd)
                        sq0 = qd * 4
                        P.dma("sp", "d_s0m%d" % qd, lambda e: e.dma_start(out=S0q, in_=ss_d[sq0:sq0 + 4, 2 * k:2 * k + 2].rearrange("s h p n -> (h p) s n")), w=QW)
                    if M:
                        s0_load(0)

                    def stage_a1(t):
                        rt = t % 2
                        Rex, Lb = Wm["Rex"][:, rt, :], Wm["Lb"][:, rt, :]
                        db = 3
                        P.pool(lambda e: e.tensor_tensor(out=Rex.rearrange("p (h k) -> p h k", k=128), in0=dA_tok[:, t * 32 + 2 * k:t * 32 + 2 * k + 2].unsqueeze(2).to_broadcast([128, 2, 128]), in1=tri[:].unsqueeze(1).to_broadcast([128, 2, 128]), op=ALU.mult), r=["dA_tok"] + CK, w=[("Rex", rt)])
                        P.pe(lambda e: e.matmul(bank(db, 256), lhsT=ustr[:], rhs=Rex, start=True, stop=True), r=[("Rex", rt)] + CK, w=[("ps", db)])
                        P.act(lambda e: e.activation(out=Lb, in_=bank(db, 256), func=AF.Exp), r=[("ps", db)], w=[("Lb", rt)])

                    def stage_a2(t):
                        rt = t % 2
                        Lb, Mt = Wm["Lb"][:, rt, :], Wm["Mt"][:, rt]
                        P.dve(lambda e: e.tensor_tensor(out=Mt, in0=Lb.rearrange("p (h k) -> p h k", k=128), in1=cbm[:, t, :].unsqueeze(1).to_broadcast([128, 2, 128]), op=ALU.mult), r=[("Lb", rt), "cbm"], w=[("Mt", rt)])

                    def stage_b(j, t):
                        rt = t % 2
                        Mt = Wm["Mt"][:, rt]
                        if M:
                            def fyi(e):
                                for hh in range(2):
                                    ins = e.matmul(psum[hh * 64:(hh + 1) * 64, 4 * 512 + j * 128:4 * 512 + (j + 1) * 128], lhsT=xdt[:, t, hh * 64:(hh + 1) * 64], rhs=Mt[:, hh, :], start=True, stop=True)
                                return ins
                            P.pe(fyi, r=["xdt", ("Mt", rt)], w=[("ps", 4)])
                        if t < 8:
                            if M:
                                sl = t % 2
                                P.pool(lambda e: e.tensor_copy(out=Wm["STb"][:, sl, :], in_=Sblk), r=[SK], w=[("STb", sl)])
                                P.pe(lambda e: e.matmul(bank(5, 128, j * 128), lhsT=Wm["STb"][:, sl, :], rhs=CcT[:, t * 128:(t + 1) * 128], start=True, stop=True), r=[("STb", sl), "CcT"], w=[("ps", 5)])
                            P.pe(lambda e: e.matmul(bank(mbk(6), 128, j * 128), lhsT=Btok[:, t, :], rhs=xdtd[:, t, :], start=True, stop=True), r=["Btok", "xdtd"], w=[("ps", mbk(6))])
                            for hh in range(2):
                                P.dve(lambda e, hh=hh: e.scalar_tensor_tensor(out=Sblk[:, hh * 64:(hh + 1) * 64], in0=Sblk[:, hh * 64:(hh + 1) * 64], scalar=eab[:, t * 32 + 2 * k + hh:t * 32 + 2 * k + hh + 1], in1=bank(mbk(6), 64, j * 128 + hh * 64), op0=ALU.mult, op1=ALU.add), r=[SK, "eab", ("ps", mbk(6))], w=[SK])
                        else:
                            s0_load(1)
                            s0_load(2)
                            for qd in range(4):
                                sq0 = qd * 4
                                qb_ = qd
                                S0q, QK, QW = s0q(qd)

                                def ftS(e, S0q=S0q):
                                    for q in range(4):
                                        ins = e.transpose(out=bank(3, 128, q * 128), in_=S0q[:, q, :], identity=identf[:])
                                    return ins
                                P.pe(ftS, r=[QK] + CK, w=[("ps", 3)])
                                P.act(lambda e: e.copy(out=Wm["S0Tb"], in_=bank(3, 512).rearrange("p (s n) -> p s n", n=128)), r=[("ps", 3)], w=["S0Tb"])

                                def fyis(e, sq0=sq0):
                                    for q in range(4):
                                        sidx = sq0 + q
                                        ins = e.matmul(bank(5, 8, sidx * 8), lhsT=Wm["S0Tb"][:, q, :], rhs=CcT[:, 1024 + sidx * 8:1024 + sidx * 8 + 8], start=True, stop=True)
                                    return ins
                                P.pe(fyis, r=["S0Tb", "CcT"], w=[("ps", 5)])
                                P.dve(lambda e, sq0=sq0: e.tensor_tensor(out=Wm["Bblk"], in0=Btok[:, 8, :].unsqueeze(1).to_broadcast([128, 4, 128]), in1=bmask[:, sq0:sq0 + 4].unsqueeze(2).to_broadcast([128, 4, 128]), op=ALU.mult), r=["Btok"] + CK, w=["Bblk"])
                                P.pe(lambda e: e.matmul(bank(2, 512), lhsT=xdtd[:, 8, :], rhs=Wm["Bblk"], start=True, stop=True), r=["xdtd", "Bblk"], w=[("ps", 2)])
                                P.dve(lambda e, sq0=sq0, S0q=S0q: e.tensor_tensor(out=S0q, in0=S0q, in1=eaxl_[:, sq0:sq0 + 4].unsqueeze(2).to_broadcast([128, 4, 128]), op=ALU.mult), r=[QK, KA], w=[QK])
                                P.dve(lambda e, S0q=S0q: e.tensor_tensor(out=S0q, in0=S0q, in1=bank(2, 512).rearrange("p (s n) -> p s n", n=128), op=ALU.add), r=[QK, ("ps", 2)], w=[QK])
                                P.dma("sp", "d_s0o%d" % qb_, lambda e, sq0=sq0, S0q=S0q: e.dma_start(out=ssm_s[sq0:sq0 + 4, 2 * k:2 * k + 2].rearrange("s h p n -> (h p) s n"), in_=S0q), r=[QK])
                                if qd == 0:
                                    s0_load(3)

                    if M:
                        stage_a1(0)
                        stage_a1(1)
                        stage_a2(0)
                    for tiles in rounds(nt):
                        r0, n_ = tiles[0], len(tiles)
                        n = n_ * 128
                        c0 = r0 * 128

                        def ftx(e, tiles=tiles):
                            for j, t in enumerate(tiles):
                                ins = e.transpose(out=bankb(mbk(2))[:, j * 128:(j + 1) * 128], in_=xcT_[:, t * 128:(t + 1) * 128], identity=identb[:])
                            return ins
                        P.pe(ftx, r=[KX] + CK, w=[("ps", mbk(2))])
                        xps = bankb(mbk(2))[:, 0:n].rearrange("p (t h q) -> p t h q", h=2, q=64)
                        P.dve(lambda e, r0=r0, n_=n_, xps=xps: e.tensor_tensor(out=xdtd[:, r0:r0 + n_, :].rearrange("p t (h q) -> p t h q", q=64), in0=xps, in1=w_tok.rearrange("p (t h) -> p t h", h=32)[:, r0:r0 + n_, 2 * k:2 * k + 2].unsqueeze(3).to_broadcast([128, n_, 2, 64]), op=ALU.mult), r=[("ps", mbk(2)), "w_tok"], w=["xdtd"])
                        if M:
                            P.dve(lambda e, r0=r0, n_=n_, xps=xps: e.tensor_tensor(out=xdt[:, r0:r0 + n_, :].rearrange("p t (h q) -> p t h q", q=64), in0=xps, in1=dt_tok.rearrange("p (t h) -> p t h", h=32)[:, r0:r0 + n_, 2 * k:2 * k + 2].unsqueeze(3).to_broadcast([128, n_, 2, 64]), op=ALU.mult), r=[("ps", 2), "dt_tok"], w=["xdt"])
                            P.pe(lambda e, c0=c0, n=n: e.matmul(bank(7, n), lhsT=eexpk[0:32, :], rhs=acumT[0:32, c0:c0 + n], start=True, stop=True), r=[KE, "acumT"], w=[("ps", 7)])
                            P.act(lambda e, n=n: e.activation(out=Wm["eax"][:, 0:n], in_=bank(7, n), func=AF.Exp), r=[("ps", 7)], w=["eax"])
                        for j, t in enumerate(tiles):
                            if M and t + 2 < nt:
                                stage_a1(t + 2)
                            if M and t + 1 < nt:
                                stage_a2(t + 1)
                            stage_b(j, t)
                        if M and r0 == 4:
                            P.pe(lambda e: e.transpose(out=bank(3, 128), in_=Sblk, identity=identf[:]), r=[SK] + CK, w=[("ps", 3)])
                            P.act(lambda e: e.copy(out=H1["stgF"], in_=bank(3, 128)), r=[("ps", 3)], w=["stgF"])
                            P.dma("sp", "d_ssp", lambda e: e.dma_start(out=ssm_p[2 * k:2 * k + 2].rearrange("h p n -> (h p) n"), in_=H1["stgF"]), r=["stgF"])
                        if M:
                            sq_ = Wm["sq"]
                            if r0 == 8:
                                t1, t2, k1, k2 = Wm["t1s"], Wm["t2s"], "t1s", "t2s"
                            else:
                                t1, t2, k1, k2 = Wm["t1"], Wm["t2"], "t1", "t2"
                            P.dve(lambda e, n=n, t1=t1: e.tensor_tensor(out=t1[:, 0:n], in0=bank(5, n), in1=Wm["eax"][:, 0:n], op=ALU.mult), r=[("ps", 5), "eax"], w=[k1])
                            P.dve(lambda e, n=n, t1=t1, t2=t2: e.tensor_tensor(out=t2[:, 0:n], in0=t1[:, 0:n], in1=bank(4, n), op=ALU.add), r=[k1, ("ps", 4)], w=[k2])
                            P.dve(lambda e, n=n, c0=c0, t1=t1, t2=t2: e.scalar_tensor_tensor(out=t1[:, 0:n], in0=xcT_[:, c0:c0 + n], scalar=cols[:, C_DSK + k:C_DSK + k + 1], in1=t2[:, 0:n], op0=ALU.mult, op1=ALU.add), r=[KX, k2] + CK, w=[k1])
                            P.dve(lambda e, n=n, c0=c0, t1=t1: e.tensor_tensor(out=yT[:, k, c0:c0 + n], in0=t1[:, 0:n], in1=zs[:, c0:c0 + n], op=ALU.mult), r=[k1, KZ], w=[("yT", k)])
                            P.act(lambda e, n=n, c0=c0: e.activation(out=sq_[:, 0:n], in_=yT[:, k, c0:c0 + n], func=AF.Square), r=[("yT", k)], w=["sq"])
                            P.pe(lambda e, n=n: e.matmul(bank(7, n), lhsT=onesb[:], rhs=sq_[:, 0:n], start=True, stop=True), r=["sq"] + CK, w=[("ps", 7)])
                            if kk == 0:
                                P.dve(lambda e, n=n, c0=c0: e.tensor_copy(out=ssqacc[:, c0:c0 + n], in_=bank(7, n)), r=[("ps", 7)], w=["R2"])
                            else:
                                P.dve(lambda e, n=n, c0=c0: e.tensor_tensor(out=ssqacc[:, c0:c0 + n], in0=ssqacc[:, c0:c0 + n], in1=bank(7, n), op=ALU.add), r=[("ps", 7), "R2"], w=["R2"])

                def gnorm():
                    if M:
                        P.act(lambda e: e.activation(out=ssqacc[:, 0:N], in_=ssqacc[:, 0:N], func=AF.Ln, scale=1.0 / 512, bias=epsc), r=["R2"], w=["R2"])
                        P.act(lambda e: e.activation(out=ssqacc[:, 0:N], in_=ssqacc[:, 0:N], func=AF.Exp, scale=-0.5), r=["R2"], w=["R2"])
                        for kk in range(4):
                            k = g * 4 + kk
                            P.dve(lambda e, k=k: e.scalar_tensor_tensor(out=yT[:, k, :], in0=yT[:, k, :], scalar=cols[:, C_SSMN + k:C_SSMN + k + 1], in1=ssqacc[:, 0:N], op0=ALU.mult, op1=ALU.mult), r=[("yT", k), "R2"] + CK, w=[("yT", k)])

                return dict(GL_B=GL_B, GL_rest=GL_rest, early=blk_early, late=blk_late, norm=gnorm)

            G = [make_group(g_) for g_ in range(4)]
            if M:
                G[0]["GL_B"]()
                G[0]["GL_rest"]()
                G[0]["early"](0, 0)
                for g_ in range(4):
                    for kk_ in range(3):
                        late_ops = P.capture(lambda: G[g_]["late"](kk_, kk_ % 2))
                        early_ops = P.capture(lambda: G[g_]["early"](kk_ + 1, (kk_ + 1) % 2))
                        P.interleave(early_ops, late_ops)
                    if g_ < 3:
                        late_ops = P.capture(lambda: G[g_]["late"](3, 1))
                        early_ops = P.capture(lambda: (G[g_ + 1]["GL_B"](), G[g_ + 1]["early"](0, 0)))
                        P.interleave(early_ops, late_ops)
                        G[g_]["norm"]()
                        G[g_ + 1]["GL_rest"]()
                    else:
                        G[g_]["late"](3, 1)
                        G[g_]["norm"]()
            else:
                for g_ in range(4):
                    G[g_]["GL_B"]()
                    G[g_]["GL_rest"]()
                    if dual:
                        G[g_]["early"](0, 0)
                        for kk_ in range(4):
                            late_ops = P.capture(lambda: G[g_]["late"](kk_, kk_ % 2))
                            early_ops = P.capture(lambda: G[g_]["early"](kk_ + 1, (kk_ + 1) % 2)) if kk_ < 3 else []
                            P.interleave(early_ops, late_ops)
                    else:
                        for kk_ in range(4):
                            G[g_]["early"](kk_, 0)
                            G[g_]["late"](kk_, 0)
            return Wm

        def carve_common():
            AR.off = 0
            hT = AR.alloc([16, NM], BF16)
            oT_ = AR.alloc([16, NM], BF16)
            yT_ = AR.alloc([16, NM], BF16)
            return hT, oT_, yT_

        hT, oT, yT = carve_common()
        work_off = AR.off

        def carve_hgrn():
            AR.off = work_off
            shared = {}
            for k in ("T1", "T2", "lf", "bb"):
                shared[k] = AR.alloc([NM], F32)
            for k in ("kkb", "qb", "vT"):
                shared[k] = AR.alloc([NM], BF16)
            for k in ("ktok", "scm", "Sbf"):
                shared[k] = AR.alloc([9, 128], BF16)
            shared["osq"] = AR.alloc([512], BF16)
            for k in ("rs", "on"):
                shared[k] = AR.alloc([512], F32)
            shared["blast"] = AR.alloc([24], F32)
            shared["bref"] = AR.alloc([16], F32)
            hand = ("kl", "km", "qm", "qn", "gs")
            sets = []
            for wi in range(2):
                if wi == 1:
                    AR.off = 18432
                Wd = dict(shared)
                for k in hand:
                    Wd[k] = AR.alloc([NM], BF16)
                Wd["vtok"] = AR.alloc([9, 128], BF16)
                Wd["elast"] = AR.alloc([24], F32)
                if wi == 0:
                    assert AR.off <= AR_WORDS, AR.off
                ky = {k: k for k in shared}
                for k in hand + ("vtok", "elast"):
                    ky[k] = (k, wi)
                Wd["_k"] = ky
                sets.append(Wd)
            S0 = AR.alloc([16, 128], F32)
            S0b = AR.alloc([16, 128], BF16)
            Vb = AR.alloc([16, 128], BF16)
            assert AR.off <= 27648, AR.off
            for Wd in sets:
                Wd["S0"], Wd["S0b"], Wd["Vb"] = S0, S0b, Vb
                for k in ("S0", "S0b", "Vb"):
                    Wd["_k"][k] = k
            return sets

        WS = carve_hgrn()
        W = WS[0]
        sa_off = 16 * NM // 2
        sa = arena_t[:, sa_off:sa_off + 2 * 16 * NM // 2]
        xt2 = sa[:, 0:4096].rearrange("p (s d) -> p s d", d=2048)
        junk = sa[:, 4096:6144]
        xn = sa[:, 6144:7168].bitcast(BF16)
        xnB = sa[:, 7168:8192].bitcast(BF16)
        ssq = small[:, 0:2]
        rstd = small[:, 2:4]
        rmt = sb("rmt", [128, NM], BF16)

        run_P = debug.get("run_P", True)
        nheads = debug.get("nheads", 16)
        if run_P:
            P.dma("sp", "d_rm", lambda e: e.dma_start(out=rmt[:, 0:TP], in_=rmp_d), w=["rm"])
            stage_A(xp_d, 8, hT, "hT", C_NMIX, xt2, xn, ssq, rstd, junk)
            P.barrier()
            bkP = dict(proj=[0, 1], v=2, tr=3, ds=3)

            def hwP(h):
                return {"f": wfetch(w_in[:, OF + h * 128:OF + (h + 1) * 128], "hp"), "v": wfetch(w_in[:, OV + h * 128:OV + (h + 1) * 128], "hp")}

            def hgrn_P():
                slots = {0: hwP(0)}
                hgrn_head(0, "P", hT, "hT", WS[0], rmt, slots[0], "early", bkP)
                for h in range(nheads):
                    late = P.capture(lambda: hgrn_head(h, "P", hT, "hT", WS[h % 2], rmt, slots[h], "late", bkP))
                    early = []
                    if h + 1 < nheads:
                        slots[h + 1] = hwP(h + 1)
                        early = P.capture(lambda: hgrn_head(h + 1, "P", hT, "hT", WS[(h + 1) % 2], rmt, slots[h + 1], "early", bkP))
                    P.interleave(early, late)
            hg_ops = P.capture(hgrn_P)
            mb_ops = P.capture(lambda: mamba("P", hT, "hT", rmt, dual=True)) if debug.get("mamba", True) else []
            P.interleave(hg_ops, mb_ops)
            P.barrier()

        def hw(h):
            return {k: wfetch(w_in[:, o + h * 128:o + (h + 1) * 128]) for k, o in (("f", OF), ("v", OV), ("q", OQ), ("g", OG))}
        slots = {0: hw(0)}
        P.dma("sp", "d_rm", lambda e: e.dma_start(out=rmt[:, 0:NM], in_=rmm_d), w=["rm"])
        stage_A(xm_d, 9, hT, "hT", C_NMIX, xt2, xn, ssq, rstd, junk)
        P.barrier()
        P.pool(lambda e: e.memset(W["bref"][:], 0.0), w=["bref"])
        P.pool(lambda e: e.memset(W["scm"][64:128, :, 0:64], 0.0), w=["scm"])

        hgrn_head(0, "M", hT, "hT", WS[0], rmt, slots[0], "early")
        for h in range(nheads):
            late = P.capture(lambda: hgrn_head(h, "M", hT, "hT", WS[h % 2], rmt, slots[h], "late"))
            early = []
            if h + 1 < nheads:
                slots[h + 1] = hw(h + 1)
                early = P.capture(lambda: hgrn_head(h + 1, "M", hT, "hT", WS[(h + 1) % 2], rmt, slots[h + 1], "early"))
            P.interleave(early, late)
        PRE_DT = None
        if debug.get("mamba", True):
            PRE_DT = wslot("all")
            P.dma("pool", "d_w%d" % PRE_DT, lambda e: e.dma_start(out=wpool[:, PRE_DT, :, 0:32], in_=w_in[:, ODT:ODT + 32].rearrange("(c p) n -> p c n", p=128)), w=[("w", PRE_DT)])
        P.barrier()
        PRE2 = None
        if debug.get("mamba", True):
            P.dve(lambda e: e.tensor_scalar(out=Sssm[:], in0=Sssm[:], scalar1=cols[:, C_FLAG:C_FLAG + 1], scalar2=None, op0=ALU.mult), r=[("Sssm", k) for k in range(16)] + CK, w=[("Sssm", k) for k in range(16)])
            WmD = mamba("M", hT, "hT", rmt, pre_dt=PRE_DT)
            PRE2 = [wfetch(w_bhg[:, 0:128]), wfetch(w_bssm[:, 0:128]), wfetch(w_in[:, OG0:OG0 + 128]), wfetch(w_in[:, OG1:OG1 + 128])]
            P.barrier()
            for k_ in dbg_d:
                if k_.startswith("M_"):
                    P.dma("sp", "d_dbg", lambda e, k_=k_: e.dma_start(out=dbg_d[k_], in_=WmD[k_[2:]]), r=[])

        blocks3 = [(0, 512), (512, 512), (1024, 128)]
        OT_K = [("oT", h) for h in range(16)]
        YT_K = [("yT", k) for k in range(16)]
        HT_K = [("hT", t) for t in range(9)]
        AR.off = work_off
        mT = AR.alloc([16, NM], BF16)
        actT = AR.alloc([4, NM], BF16)
        tmpA = AR.alloc([512], F32)
        tmpB = AR.alloc([512], F32)
        xn2 = AR.alloc([2048], BF16)
        assert AR.off <= AR_WORDS, AR.off
        junk2 = actT.rearrange("p a b -> p (a b)")[:, 0:2048]
        AK = [("actT", fc) for fc in range(4)]
        x1 = arena_t[:, 9216:27648].rearrange("p (t d) -> p t d", d=2048)
        wb2 = arena_t[:, 0:8192].bitcast(BF16).rearrange("p (s n) -> p s n", s=2)
        nfin = arena_t[:, 0:2048]
        wb2state = {"n": 0}

        def wb2_fetch(src_ap, nchunk):
            s_ = wb2state["n"] % 2
            wb2state["n"] += 1
            dstv = wb2[:, s_, :].rearrange("p (c n) -> p c n", c=nchunk)
            srcv = src_ap.rearrange("(c p) n -> p c n", p=128)
            hc = nchunk // 2
            P.dma_multi("pool", "d_wb%d" % s_, [lambda e, h_=h_: e.dma_start(out=dstv[:, hc * h_:hc * (h_ + 1), :], in_=srcv[:, hc * h_:hc * (h_ + 1), :]) for h_ in range(2)], w=[("wb2", s_)])
            return s_

        def p2_fetch(e_):
            return [wfetch(w_bhg[:, e_ * 128:(e_ + 1) * 128]), wfetch(w_bssm[:, e_ * 128:(e_ + 1) * 128]),
                    wfetch(w_in[:, OG0 + e_ * 128:OG0 + (e_ + 1) * 128]), wfetch(w_in[:, OG1 + e_ * 128:OG1 + (e_ + 1) * 128])]

        def p2_chunk(e_, pre=None):
            sl_ = pre if pre is not None else p2_fetch(e_)
            srcs = [(sl_[0], oT, OT_K), (sl_[1], yT, YT_K), (sl_[2], hT, HT_K), (sl_[3], hT, HT_K)]
            for i, (c0, n) in enumerate(blocks3):
                bs = 4 * (i % 2)
                for q, (slot, src, sk) in enumerate(srcs):
                    def f(e, q=q, slot=slot, src=src, c0=c0, n=n, bs=bs):
                        for c in range(16):
                            ins = e.matmul(bank(bs + q, n), lhsT=wpool[:, slot, c, :], rhs=src[:, c, c0:c0 + n], start=(c == 0), stop=(c == 15))
                        return ins
                    P.pe(f, r=[("w", slot)] + sk, w=[("ps", bs + q)])
                P.act(lambda e, bs=bs, n=n: e.activation(out=tmpA[:, 0:n], in_=bank(bs + 2, n), func=AF.Sigmoid), r=[("ps", bs + 2)], w=["tmpA"])
                P.act(lambda e, bs=bs, n=n: e.activation(out=tmpB[:, 0:n], in_=bank(bs + 3, n), func=AF.Sigmoid), r=[("ps", bs + 3)], w=["tmpB"])
                P.dve(lambda e, bs=bs, n=n: e.tensor_tensor(out=tmpA[:, 0:n], in0=tmpA[:, 0:n], in1=bank(bs, n), op=ALU.mult), r=["tmpA", ("ps", bs)], w=["tmpA"])
                P.dve(lambda e, bs=bs, n=n: e.tensor_tensor(out=tmpB[:, 0:n], in0=tmpB[:, 0:n], in1=bank(bs + 1, n), op=ALU.mult), r=["tmpB", ("ps", bs + 1)], w=["tmpB"])
                P.dve(lambda e, n=n, c0=c0: e.tensor_tensor(out=mT[:, e_, c0:c0 + n], in0=tmpA[:, 0:n], in1=tmpB[:, 0:n], op=ALU.add), r=["tmpA", "tmpB"], w=[("mT", t_) for t_ in range(c0 // 128, (c0 + n) // 128)])

        if debug.get("dense", True):
            for e_ in range(16):
                p2_chunk(e_, PRE2 if e_ == 0 else None)
            P.barrier()
            for t in range(9):
                P.dma("sp", "d_x1_%d" % t, lambda e, t=t: e.dma_start(out=x1[:, t, :], in_=xm_d[t * 128:(t + 1) * 128, :]), w=[("x1", t)])

            def p3_col(cb):
                s_ = wb2_fetch(w_out[:, cb * 512:(cb + 1) * 512], 16)
                wv = wb2[:, s_, :].rearrange("p (c n) -> p c n", c=16)
                for t in range(9):
                    bk = (cb * 9 + t) % 4

                    def f(e, t=t, bk=bk):
                        for c in range(16):
                            ins = e.matmul(bank(bk, 512), lhsT=mT[:, c, t * 128:(t + 1) * 128], rhs=wv[:, c, :], start=(c == 0), stop=(c == 15))
                        return ins
                    P.pe(f, r=[("wb2", s_), ("mT", t)], w=[("ps", bk)])
                    P.dve(lambda e, t=t, bk=bk: e.tensor_tensor(out=x1[:, t, cb * 512:(cb + 1) * 512], in0=x1[:, t, cb * 512:(cb + 1) * 512], in1=bank(bk, 512), op=ALU.add), r=[("ps", bk), ("x1", t)], w=[("x1", t)])
                    if cb == 3:
                        norm_T(t, x1[:, t, :], [("x1", t)], (xn2, xn2B)[t % 2], ["xn2"] if t % 2 == 0 else AK[1:4], junk2, AK[0:2], h2T, "h2T", C_NFFN, extra_w=[("mT", t)])
            h2T = mT
            xn2B = actT.rearrange("p a b -> p (a b)")[:, 2048:4096]
            for cb in range(3):
                p3_col(cb)
            PRE4 = (wfetch(w_g[:, 0:128]), wfetch(w_u[:, 0:128]))
            p3_col(3)
            H2_K = [("h2T", t) for t in range(9)]

            def ffn_group(fg):
                s_d = wb2_fetch(w_d[fg * 512:(fg + 1) * 512, :], 4)
                wdv = wb2[:, s_d, :].rearrange("p (c n) -> p c n", c=4)
                for fc in range(4):
                    f0 = fg * 512 + fc * 128
                    if fg == 0 and fc == 0:
                        s_g, s_u = PRE4
                    else:
                        s_g = wfetch(w_g[:, f0:f0 + 128])
                        s_u = wfetch(w_u[:, f0:f0 + 128])
                    for i, (c0, n) in enumerate(blocks3):
                        bg, bu = 2 * (i % 2), 2 * (i % 2) + 1

                        def fgm(e, slot=s_g, bk=bg, c0=c0, n=n):
                            for c in range(16):
                                ins = e.matmul(bank(bk, n), lhsT=wpool[:, slot, c, :], rhs=h2T[:, c, c0:c0 + n], start=(c == 0), stop=(c == 15))
                            return ins
                        P.pe(fgm, r=[("w", s_g)] + [("h2T", t_) for t_ in range(c0 // 128, (c0 + n) // 128)], w=[("ps", bg)])

                        def fum(e, slot=s_u, bk=bu, c0=c0, n=n):
                            for c in range(16):
                                ins = e.matmul(bank(bk, n), lhsT=wpool[:, slot, c, :], rhs=h2T[:, c, c0:c0 + n], start=(c == 0), stop=(c == 15))
                            return ins
                        P.pe(fum, r=[("w", s_u)] + [("h2T", t_) for t_ in range(c0 // 128, (c0 + n) // 128)], w=[("ps", bu)])
                        P.act(lambda e, bg=bg, n=n: e.activation(out=tmpA[:, 0:n], in_=bank(bg, n), func=AF.Silu), r=[("ps", bg)], w=["tmpA"])
                        P.dve(lambda e, bu=bu, n=n, c0=c0, fc=fc: e.tensor_tensor(out=actT[:, fc, c0:c0 + n], in0=tmpA[:, 0:n], in1=bank(bu, n), op=ALU.mult), r=["tmpA", ("ps", bu)], w=[("actT", fc)])
                for t in range(9):
                    for cb in range(4):
                        bk = 4 + (t * 4 + cb) % 4

                        def fd(e, t=t, cb=cb, bk=bk):
                            for fc in range(4):
                                ins = e.matmul(bank(bk, 512), lhsT=actT[:, fc, t * 128:(t + 1) * 128], rhs=wdv[:, fc, cb * 512:(cb + 1) * 512], start=(fc == 0), stop=(fc == 3))
                            return ins
                        P.pe(fd, r=[("wb2", s_d)] + [("actT", fc) for fc in range(4)], w=[("ps", bk)])
                        P.dve(lambda e, t=t, cb=cb, bk=bk: e.tensor_tensor(out=x1[:, t, cb * 512:(cb + 1) * 512], in0=x1[:, t, cb * 512:(cb + 1) * 512], in1=bank(bk, 512), op=ALU.add), r=[("ps", bk), ("x1", t)], w=[("x1", t)])
            for fg in range(11):
                ffn_group(fg)
            P.barrier()
            P.dma("sp", "d_nf", lambda e: e.dma_start(out=nfin, in_=nfin_d), w=["nfin"])
            for t in range(9):
                sl = t % 2
                P.act(lambda e, t=t, sl=sl: e.activation(out=junk2, in_=x1[:, t, :], func=AF.Square, accum_out=ssq[:, sl:sl + 1]), r=[("x1", t)], w=AK + [("ssq", sl)])
                P.act(lambda e, sl=sl: e.activation(out=rstd[:, sl:sl + 1], in_=ssq[:, sl:sl + 1], func=AF.Sqrt, scale=1.0 / D, bias=EPS), r=[("ssq", sl)], w=[("rstd", sl)])
                P.dve(lambda e, sl=sl: e.reciprocal(out=rstd[:, sl:sl + 1], in_=rstd[:, sl:sl + 1]), r=[("rstd", sl)], w=[("rstd", sl)])
                P.dve(lambda e, t=t, sl=sl: e.scalar_tensor_tensor(out=x1[:, t, :], in0=x1[:, t, :], scalar=rstd[:, sl:sl + 1], in1=nfin, op0=ALU.mult, op1=ALU.mult), r=[("x1", t), ("rstd", sl), "nfin"], w=[("x1", t)])
                if t < 8:
                    P.dma("sp", "d_yo", lambda e, t=t: e.dma_start(out=y_main[t * 128:(t + 1) * 128, :], in_=x1[:, t, :]), r=[("x1", t)])
                else:
                    P.dma("sp", "d_yo", lambda e, t=t: e.dma_start(out=y_samp, in_=x1[:, t, :]), r=[("x1", t)])

        if "oT" in dbg_d:
            P.dma("sp", "d_dbg", lambda e: e.dma_start(out=dbg_d["oT"], in_=oT[:, 0:nheads, :]), r=[("oT", h) for h in range(16)])
        for k_ in dbg_d:
            if k_.startswith("W_"):
                P.dma("sp", "d_dbg", lambda e, k_=k_: e.dma_start(out=dbg_d[k_], in_=W[k_[2:]]), r=[])
        if "Sssm" in dbg_d:
            P.dma("sp", "d_dbg", lambda e: e.dma_start(out=dbg_d["Sssm"], in_=Sssm[:]), r=[("Sssm", k) for k in range(16)])
        if "yT" in dbg_d:
            P.dma("sp", "d_dbg", lambda e: e.dma_start(out=dbg_d["yT"], in_=yT), r=[("yT", k) for k in range(16)])
        if "hT" in dbg_d:
            P.dma("sp", "d_dbg", lambda e: e.dma_start(out=dbg_d["hT"], in_=hT), r=[("hT", t) for t in range(9)])
        P.emit(nc, es)
        print("prog stats", P.stats, "arena words used", AR.off)
    return nc


def host_inputs(inputs):
    f32 = np.float32
    x_prompt = np.asarray(inputs["x_prompt"], f32)
    x_sample = np.asarray(inputs["x_sample"], f32)
    sh = np.asarray(inputs["state_hgrn"], f32)[0]
    ss = np.asarray(inputs["state_ssm"], f32)[0]
    sc = np.asarray(inputs["state_conv"], f32)[0]

    def col16(v):
        return np.asarray(v, f32).reshape(16, 128).T

    cols = np.zeros((128, NCOLS), f32)
    cols[:, C_NMIX:C_NMIX + 16] = col16(inputs["norm_mix"][0])
    cols[:, C_L0:C_L0 + 16] = col16(inputs["hg_lb_logits"][0])
    cols[:, C_L1:C_L1 + 16] = col16(inputs["hg_lb_logits"][1])
    cols[:, C_HGN:C_HGN + 16] = col16(inputs["hg_norm"][0])
    cols[:, C_SSMN:C_SSMN + 16] = col16(inputs["ssm_norm"][0])
    cols[:, C_NFFN:C_NFFN + 16] = col16(inputs["norm_ffn"][0])
    cw = np.asarray(inputs["conv_w"], f32)[0]
    cols[:, C_CW:C_CW + 96] = cw.reshape(4, 24, 128).transpose(2, 0, 1).reshape(128, 96)
    cols[:, C_CB:C_CB + 24] = np.asarray(inputs["conv_b"], f32)[0].reshape(24, 128).T
    cols[:, C_DSK:C_DSK + 16] = np.repeat(np.asarray(inputs["d_skip"], f32)[0], 64).reshape(16, 128).T
    cols[0:32, C_DTB] = np.asarray(inputs["dt_bias"], f32)[0]
    cols[0:32, C_ALOG] = np.asarray(inputs["a_log"], f32)[0]

    bf = ml_dtypes.bfloat16
    ii = np.arange(128)
    identf = np.eye(128, dtype=f32)
    maskc = (ii[:, None] <= ii[None, :]).astype(f32)
    maskb = maskc * ((ii[:, None] // 8) == (ii[None, :] // 8)).astype(f32)
    rmp = np.ones((128, TP), f32)
    rmp[:, 0::128] = 0.0
    rmm = np.ones((128, NM), f32)
    rmm[:, 0:1024:128] = 0.0
    rmm[:, 1024::8] = 0.0
    bmask = ((ii[:, None] // 8) == np.arange(16)[None, :]).astype(f32)
    ustr = (ii[:, None] > ii[None, :]).astype(f32)
    tri = (ii[:, None] <= ii[None, :]).astype(f32)
    eexp = np.repeat(np.eye(32, dtype=f32), 64, axis=1)
    shared = dict(
        w_in=np.ascontiguousarray(inputs["w_in"][0], f32), w_bhg=np.ascontiguousarray(inputs["w_branch_hg"][0], f32),
        w_bssm=np.ascontiguousarray(inputs["w_branch_ssm"][0], f32), w_out=np.ascontiguousarray(inputs["w_out"][0], f32),
        w_g=np.ascontiguousarray(inputs["w_ffn_gate"][0], f32), w_u=np.ascontiguousarray(inputs["w_ffn_up"][0], f32),
        w_d=np.ascontiguousarray(inputs["w_ffn_down"][0], f32),
        nfin=np.ascontiguousarray(np.broadcast_to(np.asarray(inputs["norm_final"], f32)[None, :], (128, D))),
        identb=identf.astype(bf), identf=identf, maskc=maskc.astype(bf), maskb=maskb.astype(bf),
        onesf=np.ones((128, 128), f32), onesb=np.ones((128, 128), f32).astype(bf), rmp=rmp.astype(bf), rmm=rmm.astype(bf), bmask=bmask.astype(bf), ustr=ustr, tri=tri, eexp=eexp,
    )
    maps = []
    for c in range(NCORES):
        b, hf = c // 2, c % 2
        m = dict(shared)
        m["xp"] = np.ascontiguousarray(x_prompt[b, 0:1024]) if hf == 1 else np.zeros((TP, D), f32)
        m["xm"] = np.ascontiguousarray(np.concatenate([x_prompt[b, hf * 1024:(hf + 1) * 1024], x_sample[16 * c:16 * c + 16].reshape(128, D)], 0))
        m["sh"] = np.ascontiguousarray(sh[16 * c:16 * c + 16])
        m["ss"] = np.ascontiguousarray(ss[16 * c:16 * c + 16])
        m["sc"] = np.ascontiguousarray(sc[16 * c:16 * c + 16])
        cc = cols.copy()
        cc[:, C_FLAG] = float(hf)
        m["cols"] = cc
        maps.append(m)
    return maps


_NC_CACHE = {}


def kernel(**inputs):
    maps = host_inputs(inputs)
    if "nc" not in _NC_CACHE:
        _NC_CACHE["nc"] = build()
    nc = _NC_CACHE["nc"]
    res = run_bass_kernel_spmd(nc, maps, core_ids=list(range(NCORES)))
    R = res.results
    f32 = np.float32
    y_prompt = np.zeros((4, 2048, D), f32)
    y_sample = np.zeros((128, 8, D), f32)
    hgp = np.zeros((1, 4, 16, 128, 128), f32)
    ssp = np.zeros((1, 4, 32, 64, 128), f32)
    cvp = np.zeros((1, 4, 3, 3072), f32)
    hgs = np.zeros((1, 128, 16, 128, 128), f32)
    sss = np.zeros((1, 128, 32, 64, 128), f32)
    cvs = np.zeros((1, 128, 3, 3072), f32)
    for c in range(NCORES):
        b, hf = c // 2, c % 2
        r = R[c]
        y_prompt[b, hf * 1024:(hf + 1) * 1024] = r["y_main"]
        y_sample[16 * c:16 * c + 16] = r["y_samp"].reshape(16, 8, D)
        if hf == 1:
            hgp[0, b] = r["hg_p"]
            ssp[0, b] = r["ssm_p"]
            cvp[0, b] = r["conv_p"]
        hgs[0, 16 * c:16 * c + 16] = r["hg_s"]
        sss[0, 16 * c:16 * c + 16] = r["ssm_s"]
        cvs[0, 16 * c:16 * c + 16] = r["conv_s"]
    return (y_prompt, y_sample, hgp, ssp, cvp, hgs, sss, cvs)
```

```python
import numpy as np
import ml_dtypes
from contextlib import ExitStack
import concourse.bass as bass
import concourse.mybir as mybir
from concourse.bass_utils import run_bass_kernel_spmd

F32 = mybir.dt.float32
BF16 = mybir.dt.bfloat16
AF = mybir.ActivationFunctionType
ALU = mybir.AluOpType

D = 2048
NCH = 16
TP = 1024
TM = 1024
TS = 128
NM = TM + TS
FFN = 5632
EPS = 1e-6
OQ, OF, OV, OG, OZ, OX, OB, OC, ODT, OG0, OG1 = 0, 2048, 4096, 6144, 8192, 10240, 12288, 12800, 13312, 13344, 15392
WIN = 17440
NCORES = 8

C_NMIX, C_L0, C_L1, C_HGN, C_SSMN, C_NFFN, C_CW, C_CB, C_DSK, C_DTB, C_ALOG, C_FLAG = 0, 16, 32, 48, 64, 80, 96, 192, 216, 232, 233, 234
NCOLS = 235


class Prog:
    ENGS = ("pe", "act", "dve", "pool", "sp")

    def __init__(self):
        self.ops = []

    def op(self, eng, fn, reads=(), writes=(), dma=None):
        self.ops.append(dict(eng=eng, fn=fn, reads=list(reads), writes=list(writes), dma=dma, bar=False))

    def pe(self, fn, r=(), w=()):
        self.op("pe", fn, r, w)

    def act(self, fn, r=(), w=()):
        self.op("act", fn, r, w)

    def dve(self, fn, r=(), w=()):
        self.op("dve", fn, r, w)

    def pool(self, fn, r=(), w=()):
        self.op("pool", fn, r, w)

    def dma(self, eng, sem, fn, r=(), w=()):
        self.op(eng, fn, r, w, dma=sem)

    def dma_multi(self, eng, sem, fns, r=(), w=()):
        self.op(eng, list(fns), r, w, dma=sem)

    def capture(self, fn):
        n0 = len(self.ops)
        fn()
        got = self.ops[n0:]
        del self.ops[n0:]
        return got

    def interleave(self, a, b):
        ia = ib = 0
        na, nb = len(a), len(b)
        while ia < na or ib < nb:
            if ib >= nb or (ia < na and ia * nb <= ib * na):
                self.ops.append(a[ia])
                ia += 1
            else:
                self.ops.append(b[ib])
                ib += 1

    def barrier(self):
        for e in ("act", "dve", "pool", "sp", "pe"):
            self.ops.append(dict(eng=e, fn=None, reads=[], writes=[], dma=None, bar=True))

    def analyze(self):
        ops = self.ops
        last_w = {}
        readers = {}
        last_eng = {}
        open_dma = []
        for i, o in enumerate(ops):
            deps = set()
            if o["bar"]:
                for e, j in last_eng.items():
                    deps.add(j)
                deps.update(open_dma)
            for k in o["reads"]:
                if k in last_w:
                    deps.add(last_w[k])
            for k in o["writes"]:
                if k in last_w:
                    deps.add(last_w[k])
                for r in readers.get(k, ()):
                    deps.add(r)
            deps.discard(i)
            o["deps"] = deps
            for k in o["reads"]:
                readers.setdefault(k, []).append(i)
            for k in o["writes"]:
                last_w[k] = i
                readers[k] = []
            if o["fn"] is not None:
                if o["dma"] is not None:
                    open_dma.append(i)
                else:
                    last_eng[o["eng"]] = i
        need = [False] * len(ops)
        for i, o in enumerate(ops):
            for j in o["deps"]:
                dj = ops[j]
                if dj["dma"] is None and dj["eng"] == "pe" and o["eng"] == "pe" and o["dma"] is None and not o["bar"]:
                    continue
                need[j] = True
        cnt = {}
        clock = {e: {} for e in self.ENGS}
        for i, o in enumerate(ops):
            E = o["eng"]
            clk = clock[E]
            waits = {}
            for j in sorted(o["deps"]):
                dj = ops[j]
                if dj["dma"] is None and dj["eng"] == "pe" and E == "pe" and o["dma"] is None and not o["bar"]:
                    continue
                s, c = dj["signal"]
                if clk.get(s, 0) < c:
                    waits[s] = max(waits.get(s, 0), c)
                    for k, v in dj["clock"].items():
                        if clk.get(k, 0) < v:
                            clk[k] = v
            o["waits"] = waits
            if o["fn"] is None:
                o["signal"] = None
                o["clock"] = None
            elif o["dma"] is not None:
                s = o["dma"]
                cnt[s] = cnt.get(s, 0) + 16 * (len(o["fn"]) if isinstance(o["fn"], list) else 1)
                o["signal"] = (s, cnt[s])
                ck = dict(clk)
                ck[s] = cnt[s]
                o["clock"] = ck
            elif need[i]:
                s = "E_" + E
                cnt[s] = cnt.get(s, 0) + 1
                o["signal"] = (s, cnt[s])
                ck = dict(clk)
                ck[s] = cnt[s]
                o["clock"] = ck
            else:
                o["signal"] = None
                o["clock"] = None
        return cnt

    def emit(self, nc, es):
        cnt = self.analyze()
        sems = {}
        for s in sorted(cnt):
            sems[s] = es.enter_context(nc.semaphore(s))
        block = es.enter_context(nc.Block())
        ops = self.ops

        def run(engname):
            def body(eng):
                for o in ops:
                    if o["eng"] != engname:
                        continue
                    for s, c in o["waits"].items():
                        eng.wait_ge(sems[s], c)
                    if o["fn"] is None:
                        continue
                    if isinstance(o["fn"], list):
                        for f_ in o["fn"]:
                            f_(eng).then_inc(sems[o["signal"][0]], 16)
                        continue
                    ins = o["fn"](eng)
                    if o["signal"] is not None:
                        s, c = o["signal"]
                        ins.then_inc(sems[s], 16 if o["dma"] is not None else 1)
                if engname == "sp":
                    for s, c in cnt.items():
                        if not s.startswith("E_"):
                            eng.wait_ge(sems[s], c)

            return body

        block.tensor(run("pe"))
        block.scalar(run("act"))
        block.vector(run("dve"))
        block.gpsimd(run("pool"))
        block.sync(run("sp"))
        self.stats = {e: sum(1 for o in ops if o["eng"] == e and o["fn"] is not None) for e in self.ENGS}
        self.stats["waits"] = sum(len(o["waits"]) for o in ops)
        self.stats["sems"] = len(cnt)
        self.stats["maxcnt"] = max(cnt.values())


class Arena:
    def __init__(self, ap):
        self.ap = ap
        self.off = 0

    def alloc(self, free_shape, dtype):
        n = 1
        for s in free_shape:
            n *= s
        nbytes = n * (4 if dtype == F32 else 2)
        nwords = (nbytes + 3) // 4
        nwords = (nwords + 7) // 8 * 8
        v = self.ap[:, self.off:self.off + nwords]
        self.off += nwords
        if dtype != F32:
            v = v.bitcast(dtype)
        v = v[:, 0:n]
        if len(free_shape) == 2:
            v = v.rearrange("p (a b) -> p a b", b=free_shape[1])
        elif len(free_shape) == 3:
            v = v.rearrange("p (a b c) -> p a b c", b=free_shape[1], c=free_shape[2])
        return v


def build(debug=None):
    debug = debug or {}
    nc = bass.Bass("TRN2", target_bir_lowering=False)

    def din(name, shape, dt=F32):
        return nc.dram_tensor(name, list(shape), dt, kind="ExternalInput").ap()

    def dout(name, shape, dt=F32):
        return nc.dram_tensor(name, list(shape), dt, kind="ExternalOutput").ap()

    xp_d = din("xp", [TP, D])
    xm_d = din("xm", [NM, D])
    sh_d = din("sh", [16, 16, 128, 128])
    ss_d = din("ss", [16, 32, 64, 128])
    sc_d = din("sc", [16, 3, 3072])
    cols_d = din("cols", [128, NCOLS])
    w_in = din("w_in", [D, WIN])
    w_bhg = din("w_bhg", [D, D])
    w_bssm = din("w_bssm", [D, D])
    w_out = din("w_out", [D, D])
    w_g = din("w_g", [D, FFN])
    w_u = din("w_u", [D, FFN])
    w_d = din("w_d", [FFN, D])
    nfin_d = din("nfin", [128, D])
    identb_d = din("identb", [128, 128], BF16)
    identf_d = din("identf", [128, 128])
    maskc_d = din("maskc", [128, 128], BF16)
    maskb_d = din("maskb", [128, 128], BF16)
    onesf_d = din("onesf", [128, 128])
    onesb_d = din("onesb", [128, 128], BF16)
    rmp_d = din("rmp", [128, TP], BF16)
    rmm_d = din("rmm", [128, NM], BF16)
    bmask_d = din("bmask", [128, 16], BF16)
    ustr_d = din("ustr", [128, 128])
    tri_d = din("tri", [128, 128])
    eexp_d = din("eexp", [32, 2048])

    y_main = dout("y_main", [TM, D])
    y_samp = dout("y_samp", [TS, D])
    hg_p = dout("hg_p", [16, 128, 128])
    ssm_p = dout("ssm_p", [32, 64, 128])
    conv_p = dout("conv_p", [3, 3072])
    hg_s = dout("hg_s", [16, 16, 128, 128])
    ssm_s = dout("ssm_s", [16, 32, 64, 128])
    conv_s = dout("conv_s", [16, 3, 3072])
    dbg_d = {k: dout("dbg_" + k, v[0], v[1]) for k, v in debug.get("outs", {}).items()}

    P = Prog()
    es = ExitStack()
    with es:
        def sb(name, shape, dt=F32):
            return es.enter_context(nc.sbuf_tensor("s_" + name, list(shape), dt))

        psum = es.enter_context(nc.psum_tensor("psum", [128, 4096], F32))

        def bank(b, n=512, off=0):
            return psum[:, b * 512 + off: b * 512 + off + n]

        def bankb(b):
            return psum[:, b * 512:(b + 1) * 512].bitcast(BF16)

        cols = sb("cols", [128, NCOLS])
        identb = sb("identb", [128, 128], BF16)
        identf = sb("identf", [128, 128])
        maskc = sb("maskc", [128, 128], BF16)
        maskb = sb("maskb", [128, 128], BF16)
        onesf = sb("onesf", [128, 128])
        onesb = sb("onesb", [128, 128], BF16)
        bmask = sb("bmask", [128, 16], BF16)
        ustr = sb("ustr", [128, 128])
        tri = sb("tri", [128, 128])
        convh = sb("convh", [128, 24, 3])
        lbc = sb("lbc", [128, 16])
        omlc = sb("omlc", [128, 16])
        nomlc = sb("nomlc", [128, 16])
        Shg = sb("Shg", [128, 16, 128])
        Sssm = sb("Sssm", [128, 2048])
        NWS = 5
        wpool = sb("wpool", [128, NWS, 16, 128], BF16)
        small = sb("small", [128, 64])
        AR_WORDS = 41900
        arena_t = sb("arena", [128, AR_WORDS])
        AR = Arena(arena_t[:])

        cload = [(cols, cols_d), (identb, identb_d), (identf, identf_d), (maskc, maskc_d), (maskb, maskb_d),
                 (onesf, onesf_d), (bmask, bmask_d), (ustr, ustr_d), (tri, tri_d), (onesb, onesb_d)]
        for i, (t, d_) in enumerate(cload):
            P.dma("sp", "d_c%d" % i, lambda e, t=t, d_=d_: e.dma_start(out=t[:], in_=d_), w=[("c", i)])
        CK = [("c", i) for i in range(len(cload))]
        P.dve(lambda e: e.tensor_tensor(out=lbc[:], in0=cols[:, C_L0:C_L0 + 16], in1=cols[:, C_L1:C_L1 + 16], op=ALU.subtract), r=CK, w=["lbc"])
        P.act(lambda e: e.activation(out=omlc[:], in_=lbc[:], func=AF.Sigmoid, scale=-1.0), r=["lbc"], w=["omlc"])
        P.act(lambda e: e.activation(out=lbc[:], in_=lbc[:], func=AF.Sigmoid), r=["lbc", "omlc"], w=["lbc"])
        P.dve(lambda e: e.tensor_scalar(out=nomlc[:], in0=omlc[:], scalar1=-1.0, scalar2=None, op0=ALU.mult), r=["omlc"], w=["nomlc"])
        P.pool(lambda e: e.memset(Shg[:], 0.0), w=[("Shg", h) for h in range(16)])
        P.pool(lambda e: e.memset(Sssm[:], 0.0), w=[("Sssm", k) for k in range(16)])
        P.pool(lambda e: e.memset(convh[:], 0.0), w=["convh"])
        epsc_t = sb("epsc", [128, 1])
        epsc = epsc_t[:, 0:1]
        P.pool(lambda e: e.memset(epsc_t[:], EPS), w=["epsc"])
        PS6 = [("ps", 6)]

        wstate = {"all": [list(range(NWS)), 0], "hp": [[0, 1, 2], 0], "mp": [[3, 4], 0]}

        def wslot(pool="all"):
            lst, n = wstate[pool]
            wstate[pool][1] = n + 1
            return lst[n % len(lst)]

        def wfetch(src_ap, pool="all"):
            s = wslot(pool)
            srcv = src_ap.rearrange("(c p) n -> p c n", p=128)
            P.dma_multi("pool", "d_w%d" % s,
                        [lambda e, s=s, h_=h_: e.dma_start(out=wpool[:, s, 8 * h_:8 * h_ + 8, :], in_=srcv[:, 8 * h_:8 * h_ + 8, :]) for h_ in range(2)],
                        w=[("w", s)])
            return s

        def norm_T(t, src, src_keys, xnb, xn_keys, junk_ap, junk_keys, hT, hkey, ncol_off, extra_w=()):
            sl = t % 2
            ba, bd = (4, 5) if sl == 0 else (6, 7)
            P.act(lambda e: e.activation(out=junk_ap, in_=src, func=AF.Square, accum_out=ssq[:, sl:sl + 1]), r=src_keys, w=junk_keys + [("ssq", sl)])
            P.act(lambda e: e.activation(out=rstd[:, sl:sl + 1], in_=ssq[:, sl:sl + 1], func=AF.Sqrt, scale=1.0 / D, bias=EPS), r=[("ssq", sl)], w=[("rstd", sl)])
            P.dve(lambda e: e.reciprocal(out=rstd[:, sl:sl + 1], in_=rstd[:, sl:sl + 1]), r=[("rstd", sl)], w=[("rstd", sl)])
            P.dve(lambda e: e.tensor_scalar(out=xnb, in0=src, scalar1=rstd[:, sl:sl + 1], scalar2=None, op0=ALU.mult), r=src_keys + [("rstd", sl)], w=xn_keys)

            def ftr(e):
                for c in range(16):
                    b = ba if c < 8 else bd
                    ins = e.transpose(out=bankb(b)[:, (c % 8) * 128:(c % 8 + 1) * 128], in_=xnb[:, c * 128:(c + 1) * 128], identity=identb[:])
                return ins
            P.pe(ftr, r=xn_keys + CK, w=[("ps", ba), ("ps", bd)])
            for c in range(16):
                b = ba if c < 8 else bd
                src_ps = bankb(b)[:, (c % 8) * 128:(c % 8 + 1) * 128]
                dst = hT[:, c, t * 128:(t + 1) * 128]
                nw = cols[:, ncol_off + c:ncol_off + c + 1]
                if c < 8:
                    P.act(lambda e, src_ps=src_ps, dst=dst, nw=nw: e.mul(out=dst, in_=src_ps, mul=nw), r=[("ps", b)] + CK, w=[(hkey, t)] + list(extra_w))
                else:
                    P.dve(lambda e, src_ps=src_ps, dst=dst, nw=nw: e.tensor_scalar(out=dst, in0=src_ps, scalar1=nw, scalar2=None, op0=ALU.mult), r=[("ps", b)] + CK, w=[(hkey, t)] + list(extra_w))

        def stage_A(xsrc, ntiles, hT, hkey, ncol_off, xt2, xn, ssq, rstd, junk):
            for t in range(ntiles):
                sl = t % 2
                P.dma("sp", "d_x%d" % sl, lambda e, t=t, sl=sl: e.dma_start(out=xt2[:, sl], in_=xsrc[t * 128:(t + 1) * 128, :]), w=[("xt", sl)])
                norm_T(t, xt2[:, sl], [("xt", sl)], (xn, xnB)[sl], [("xn", sl)], junk, ["junk"], hT, hkey, ncol_off)

        def hgrn_head(h, ph, hT, hkey, W, rm, wslots, part="both", bk=None):
            M = ph == "M"
            KY = W["_k"]
            bk = bk or dict(proj=[0, 1, 2], v=3, tr=5, ds=6)
            BV, BT, BD = bk["v"], bk["tr"], bk["ds"]
            EARLY = part in ("early", "both")
            LATE = part in ("late", "both")
            N = NM if M else TP
            nt = 9 if M else 8
            blocks = [(0, 512), (512, 512)] + ([(1024, 128)] if M else [])
            sf, sv = wslots["f"], wslots["v"]
            hk = [(hkey, t) for t in range(nt)]
            lb = lbc[:, h:h + 1]
            oml = omlc[:, h:h + 1]
            noml = nomlc[:, h:h + 1]

            def proj_fm(slot, blk, bi):
                c0, n = blk

                def f(e):
                    for c in range(16):
                        ins = e.matmul(bank(bi, n), lhsT=wpool[:, slot, c, :], rhs=hT[:, c, c0:c0 + n], start=(c == 0), stop=(c == 15))
                    return ins
                P.pe(f, r=[("w", slot)] + [(hkey, t) for t in range(c0 // 128, (c0 + n) // 128)], w=[("ps", bi)])

            blast = W["blast"]
            elast = W["elast"]
            lf = W["lf"]
            bb = W["bb"]
            bbm = bb[:, 0:1024].rearrange("p (t k) -> p t k", k=128)
            bbs = bb[:, 1024:1152].rearrange("p (s k) -> p s k", k=8)
            if EARLY:
                sq, sg = wslots.get("q"), wslots.get("g")

                def secA():
                    for i, blk in enumerate(blocks):
                        bi = bk["proj"][i % len(bk["proj"])]
                        proj_fm(sf, blk, bi)
                        P.act(lambda e, blk=blk, bi=bi: e.activation(out=W["T1"][:, blk[0]:blk[0] + blk[1]], in_=bank(bi, blk[1]), func=AF.Sigmoid), r=[("ps", bi)], w=[KY["T1"]])

                def secB():
                    P.act(lambda e: e.activation(out=W["lf"][:, 0:N], in_=W["T1"][:, 0:N], func=AF.Ln, bias=lb, scale=oml), r=[KY["T1"], "lbc", "omlc"], w=[KY["lf"]])
                    P.dve(lambda e: e.tensor_scalar(out=W["kkb"][:, 0:N], in0=W["T1"][:, 0:N], scalar1=noml, scalar2=oml, op0=ALU.mult, op1=ALU.add), r=[KY["T1"], "nomlc", "omlc"], w=[KY["kkb"]])
                    P.dve(lambda e: e.tensor_tensor_scan(out=W["bb"][:, 0:N], data0=rm[:, 0:N], data1=W["lf"][:, 0:N], initial=0.0, op0=ALU.mult, op1=ALU.add), r=[KY["lf"], "rm"], w=[KY["bb"]])
                    bb = W["bb"]
                    bbm = bb[:, 0:1024].rearrange("p (t k) -> p t k", k=128)
                    blast = W["blast"]
                    elast = W["elast"]
                    P.dve(lambda e: e.tensor_copy(out=blast[:, 0:8], in_=bbm[:, :, 127]), r=[KY["bb"]], w=[KY["blast"]])
                    if M:
                        bbs = bb[:, 1024:1152].rearrange("p (s k) -> p s k", k=8)
                        P.dve(lambda e: e.tensor_copy(out=blast[:, 8:24], in_=bbs[:, :, 7]), r=[KY["bb"]], w=[KY["blast"]])
                        P.dve(lambda e: e.tensor_copy(out=W["bref"][:, 0:8], in_=bbm[:, :, 63]), r=[KY["bb"]], w=[KY["bref"]])
                    nb = 24 if M else 8
                    P.act(lambda e: e.activation(out=elast[:, 0:nb], in_=blast[:, 0:nb], func=AF.Exp), r=[KY["blast"]], w=[KY["elast"]])
                    lf = W["lf"]
                    P.dve(lambda e: e.tensor_tensor(out=lf[:, 0:1024].rearrange("p (t k) -> p t k", k=128), in0=blast[:, 0:8].unsqueeze(2).to_broadcast([128, 8, 128]), in1=bbm, op=ALU.subtract), r=[KY["bb"], KY["blast"]], w=[KY["lf"]])
                    if M:
                        P.dve(lambda e: e.tensor_tensor(out=lf[:, 1024:1152].rearrange("p (s k) -> p s k", k=8), in0=blast[:, 8:24].unsqueeze(2).to_broadcast([128, 16, 8]), in1=bbs, op=ALU.subtract), r=[KY["bb"], KY["blast"]], w=[KY["lf"]])
                    P.act(lambda e: e.activation(out=W["T2"][:, 0:N], in_=lf[:, 0:N], func=AF.Exp), r=[KY["lf"]], w=[KY["T2"]])
                    P.dve(lambda e: e.tensor_tensor(out=W["kl"][:, 0:N], in0=W["kkb"][:, 0:N], in1=W["T2"][:, 0:N], op=ALU.mult), r=[KY["kkb"], KY["T2"]], w=[KY["kl"]])

                def secC():
                    for i, blk in enumerate(blocks):
                        bi = bk["proj"][i % len(bk["proj"])]
                        proj_fm(sq, blk, bi)
                        P.act(lambda e, blk=blk, bi=bi: e.activation(out=W["qb"][:, blk[0]:blk[0] + blk[1]], in_=bank(bi, blk[1]), func=AF.Copy, scale=float(128 ** -0.5)), r=[("ps", bi)], w=[KY["qb"]])

                def secD():
                    for i, blk in enumerate(blocks):
                        bi = bk["proj"][i % len(bk["proj"])]
                        proj_fm(sg, blk, bi)
                        P.act(lambda e, blk=blk, bi=bi: e.activation(out=W["gs"][:, blk[0]:blk[0] + blk[1]], in_=bank(bi, blk[1]), func=AF.Silu), r=[("ps", bi)], w=[KY["gs"]])

                def secE():
                    P.dve(lambda e: e.tensor_tensor(out=lf[:, 0:N].rearrange("p (t k) -> p t k", k=128), in0=bb[:, 0:N].rearrange("p (t k) -> p t k", k=128), in1=W["bref"][:, 0:9].unsqueeze(2).to_broadcast([128, 9, 128]), op=ALU.subtract), r=[KY["bb"], KY["bref"], KY["T2"]], w=[KY["lf"]])
                    P.act(lambda e: e.activation(out=W["T1"][:, 0:N], in_=lf[:, 0:N], func=AF.Exp), r=[KY["lf"], KY["kkb"]], w=[KY["T1"]])
                    P.dve(lambda e: e.tensor_tensor(out=W["qm"][:, 0:N], in0=W["qb"][:, 0:N], in1=W["T1"][:, 0:N], op=ALU.mult), r=[KY["qb"], KY["T1"]], w=[KY["qm"]])
                    P.act(lambda e: e.activation(out=W["T2"][:, 0:N], in_=lf[:, 0:N], func=AF.Exp, scale=-1.0), r=[KY["lf"], KY["kl"]], w=[KY["T2"]])
                    P.dve(lambda e: e.tensor_tensor(out=W["km"][:, 0:N], in0=W["kkb"][:, 0:N], in1=W["T2"][:, 0:N], op=ALU.mult), r=[KY["kkb"], KY["T2"]], w=[KY["km"]])
                    P.act(lambda e: e.activation(out=W["T1"][:, 0:N], in_=bb[:, 0:N], func=AF.Exp), r=[KY["bb"], KY["qm"]], w=[KY["T1"]])
                    P.dve(lambda e: e.tensor_tensor(out=W["qn"][:, 0:N], in0=W["qb"][:, 0:N], in1=W["T1"][:, 0:N], op=ALU.mult), r=[KY["qb"], KY["T1"]], w=[KY["qn"]])

                def secF():
                    vT = W["vT"]
                    for i, blk in enumerate(blocks):
                        bi = bk["proj"][i % len(bk["proj"])]
                        proj_fm(sv, blk, bi)
                        P.act(lambda e, blk=blk, bi=bi: e.copy(out=vT[:, blk[0]:blk[0] + blk[1]], in_=bank(bi, blk[1])), r=[("ps", bi)], w=[KY["vT"]])
                    for r0 in range(0, nt, 8):
                        tiles = list(range(r0, min(nt, r0 + 8)))

                        def fvt(e, tiles=tiles):
                            for j, t in enumerate(tiles):
                                ins = e.transpose(out=bankb(BV)[:, j * 128:(j + 1) * 128], in_=vT[:, t * 128:(t + 1) * 128], identity=identb[:])
                            return ins
                        P.pe(fvt, r=[KY["vT"]] + CK, w=[("ps", BV)])
                        n = len(tiles) * 128
                        P.act(lambda e, r0=r0, n=n: e.copy(out=W["vtok"][:, r0:r0 + n // 128, :], in_=bankb(BV)[:, 0:n].rearrange("p (t k) -> p t k", k=128)), r=[("ps", BV)], w=[KY["vtok"]])

                secA()
                if M:
                    P.interleave(P.capture(lambda: (secC(), secD())), P.capture(secB))
                    P.interleave(P.capture(secF), P.capture(secE))
                else:
                    P.interleave(P.capture(secF), P.capture(secB))
            if not LATE:
                return
            for r0 in range(0, nt, 4):
                tiles = list(range(r0, min(nt, r0 + 4)))
                n = len(tiles) * 128
                samp = (tiles[0] == 8)
                def ftr(e, tiles=tiles):
                    for j, t in enumerate(tiles):
                        ins = e.transpose(out=bankb(BT)[:, j * 128:(j + 1) * 128], in_=W["kl"][:, t * 128:(t + 1) * 128], identity=identb[:])
                    return ins
                P.pe(ftr, r=[KY["kl"]] + CK, w=[("ps", BT)])
                P.act(lambda e, r0=r0, n=n: e.copy(out=W["ktok"][:, r0:r0 + n // 128, :], in_=bankb(BT)[:, 0:n].rearrange("p (t k) -> p t k", k=128)), r=[("ps", BT)], w=[KY["ktok"]])
                if M:
                    def fsc(e, tiles=tiles):
                        for j, t in enumerate(tiles):
                            e.matmul(psum[0:64, 4 * 512 + j * 128:4 * 512 + (j + 1) * 128], lhsT=W["km"][:, t * 128:t * 128 + 64], rhs=W["qm"][:, t * 128:(t + 1) * 128], start=True, stop=True)
                            ins = e.matmul(psum[64:128, 4 * 512 + j * 128 + 64:4 * 512 + (j + 1) * 128], lhsT=W["km"][:, t * 128 + 64:(t + 1) * 128], rhs=W["qm"][:, t * 128 + 64:(t + 1) * 128], start=True, stop=True)
                        return ins
                    P.pe(fsc, r=[KY["km"], KY["qm"]], w=[("ps", 4)])
                    mk = maskb if samp else maskc
                    nt_ = n // 128
                    P.dve(lambda e, r0=r0, nt_=nt_, mk=mk: e.tensor_tensor(out=W["scm"][0:64, r0:r0 + nt_, :], in0=psum[0:64, 4 * 512:4 * 512 + nt_ * 128].rearrange("p (t k) -> p t k", k=128), in1=mk[0:64, :].unsqueeze(1).to_broadcast([64, nt_, 128]), op=ALU.mult), r=[("ps", 4)] + CK, w=[KY["scm"]])
                    P.dve(lambda e, r0=r0, nt_=nt_, mk=mk: e.tensor_tensor(out=W["scm"][64:128, r0:r0 + nt_, 64:128], in0=psum[64:128, 4 * 512:4 * 512 + nt_ * 128].rearrange("p (t k) -> p t k", k=128)[:, :, 64:128], in1=mk[64:128, 64:128].unsqueeze(1).to_broadcast([64, nt_, 64]), op=ALU.mult), r=[("ps", 4)] + CK, w=[KY["scm"]])
                if not samp:
                    def fds(e, tiles=tiles):
                        for j, t in enumerate(tiles):
                            ins = e.matmul(bank(BD, 128, j * 128), lhsT=W["ktok"][:, t, :], rhs=W["vtok"][:, t, :], start=True, stop=True)
                        return ins
                    P.pe(fds, r=[KY["ktok"], KY["vtok"]], w=[("ps", BD)])
                    for j, t in enumerate(tiles):
                        if M:
                            P.act(lambda e, t=t: e.copy(out=W["Sbf"][:, t, :], in_=Shg[:, h, :]), r=[("Shg", h)], w=[("Sbf", t)])
                        P.dve(lambda e, j=j, t=t: e.scalar_tensor_tensor(out=Shg[:, h, :], in0=Shg[:, h, :], scalar=elast[:, t:t + 1], in1=bank(BD, 128, j * 128), op0=ALU.mult, op1=ALU.add), r=[("Shg", h), KY["elast"], ("ps", BD)], w=[("Shg", h)])
                    if M and tiles[-1] == 7:
                        P.dma("sp", "d_hgp", lambda e: e.dma_start(out=hg_p[h], in_=Shg[:, h, :]), r=[("Shg", h)])
                else:
                    S0 = W["S0"]
                    S0b = W["S0b"]
                    Vb = W["Vb"]
                    P.dma("sp", "d_s0", lambda e: e.dma_start(out=S0, in_=sh_d[:, h].rearrange("s d v -> d s v")), r=[], w=[KY["S0"]])
                    P.act(lambda e: e.copy(out=S0b, in_=S0), r=[KY["S0"]], w=[KY["S0b"]])
                    P.dve(lambda e: e.tensor_tensor(out=Vb, in0=W["vtok"][:, 8, :].unsqueeze(1).to_broadcast([128, 16, 128]), in1=bmask[:].unsqueeze(2).to_broadcast([128, 16, 128]), op=ALU.mult), r=[KY["vtok"]] + CK, w=[KY["Vb"]])

                    P.dve(lambda e: e.tensor_tensor(out=S0, in0=S0, in1=elast[:, 8:24].unsqueeze(2).to_broadcast([128, 16, 128]), op=ALU.mult), r=[KY["S0"], KY["elast"], KY["S0b"]], w=[KY["S0"]])
                    for half in range(2):
                        def fbl(e, half=half):
                            for u in range(2):
                                ins = e.matmul(bank(4 + u, 512), lhsT=W["ktok"][:, 8, :], rhs=Vb[:, 8 * half + 4 * u:8 * half + 4 * u + 4, :], start=True, stop=True)
                            return ins
                        P.pe(fbl, r=[KY["ktok"], KY["Vb"]], w=[("ps", 4), ("ps", 5)])
                        P.dve(lambda e, half=half: e.tensor_tensor(out=S0[:, 8 * half:8 * half + 8, :], in0=S0[:, 8 * half:8 * half + 8, :], in1=psum[:, 2048:3072].rearrange("p (s v) -> p s v", v=128), op=ALU.add), r=[KY["S0"], ("ps", 4), ("ps", 5)], w=[KY["S0"]])
                    P.dma("sp", "d_hgs", lambda e: e.dma_start(out=hg_s[:, h].rearrange("s d v -> d s v"), in_=S0), r=[KY["S0"]])
                if M:
                    def fo(e, tiles=tiles, samp=samp):
                        for j, t in enumerate(tiles):
                            o_ = bank(7, 128, j * 128)
                            ins = e.matmul(o_, lhsT=W["vtok"][:, t, :], rhs=W["scm"][:, t, :], start=True, stop=samp)
                            if not samp:
                                ins = e.matmul(o_, lhsT=W["Sbf"][:, t, :], rhs=W["qn"][:, t * 128:(t + 1) * 128], start=False, stop=True)
                        return ins
                    P.pe(fo, r=[KY["vtok"], KY["scm"], KY["qn"]] + [("Sbf", t) for t in tiles], w=[("ps", 7)])
                    if samp:
                        def fos(e):
                            for s_ in range(16):
                                ins = e.matmul(bank(5, 8, 8 * s_), lhsT=W["S0b"][:, s_, :], rhs=W["qn"][:, 1024 + 8 * s_:1024 + 8 * s_ + 8], start=True, stop=True)
                            return ins
                        P.pe(fos, r=[KY["qn"], KY["S0b"]], w=[("ps", 5)])
                        P.act(lambda e: e.copy(out=W["on"][:, 0:128], in_=bank(5, 128)), r=[("ps", 5)], w=[KY["on"]])
                        P.dve(lambda e: e.tensor_tensor(out=W["on"][:, 0:128], in0=W["on"][:, 0:128], in1=bank(7, 128), op=ALU.add), r=[KY["on"], ("ps", 7)], w=[KY["on"]])
                    c0 = r0 * 128
                    osq, rs, on = W["osq"], W["rs"], W["on"]
                    osrc = on[:, 0:n] if samp else bank(7, n)
                    okey = KY["on"] if samp else ("ps", 7)
                    P.act(lambda e, n=n, osrc=osrc: e.activation(out=osq[:, 0:n], in_=osrc, func=AF.Square), r=[okey], w=[KY["osq"]])
                    P.pe(lambda e, n=n: e.matmul(bank(6, n), lhsT=onesb[:], rhs=osq[:, 0:n], start=True, stop=True), r=[KY["osq"]] + CK, w=PS6)
                    P.act(lambda e, n=n: e.activation(out=rs[:, 0:n], in_=bank(6, n), func=AF.Ln, scale=1.0 / 128, bias=epsc), r=PS6, w=[KY["rs"]])
                    P.act(lambda e, n=n: e.activation(out=rs[:, 0:n], in_=rs[:, 0:n], func=AF.Exp, scale=-0.5), r=[KY["rs"]], w=[KY["rs"]])
                    P.dve(lambda e, n=n, osrc=osrc: e.tensor_tensor(out=on[:, 0:n], in0=osrc, in1=rs[:, 0:n], op=ALU.mult), r=[okey, KY["rs"]], w=[KY["on"]])
                    P.dve(lambda e, n=n, c0=c0: e.scalar_tensor_tensor(out=oT[:, h, c0:c0 + n], in0=on[:, 0:n], scalar=cols[:, C_HGN + h:C_HGN + h + 1], in1=W["gs"][:, c0:c0 + n], op0=ALU.mult, op1=ALU.mult), r=[KY["on"], KY["gs"]] + CK, w=[("oT", h)])

        def carve_mamba(base=None, limit=None):
            full = base is None
            AR.off = work_off if base is None else base
            Wm = {}
            for k_ in ("acumT", "R1", "R2"):
                Wm[k_] = AR.alloc([NM], F32)
            for k_ in ("dt_tok", "dA_tok", "w_tok"):
                Wm[k_] = AR.alloc([288], F32)
            Wm["eab"] = AR.alloc([256], F32)
            Wm["alastT"] = AR.alloc([24], F32)
            for k_ in ("BcT", "CcT", "xcT"):
                Wm[k_] = AR.alloc([NM], BF16)
            for k_ in ("Btok", "xdtd"):
                Wm[k_] = AR.alloc([9, 128], BF16)
            dummy = Wm["R1"]
            if not full:
                Wm["xcT2"] = AR.alloc([NM], BF16)
            if full:
                Wm["raws"] = AR.alloc([176], F32)
                for k_ in ("cbm", "xdt"):
                    Wm[k_] = AR.alloc([9, 128], BF16)
                Wm["Bblk"] = AR.alloc([4, 128], BF16)
                Wm["S0Tb"] = AR.alloc([4, 128], BF16)
                Wm["eax"] = AR.alloc([512], F32)
                Wm["zs"] = AR.alloc([NM], BF16)
                Wm["Rex"] = AR.alloc([2, 256], F32)
                Wm["Lb"] = AR.alloc([2, 256], BF16)
                Wm["Mt"] = AR.alloc([2, 2, 128], BF16)
                Wm["t1"] = AR.alloc([512], F32)
                Wm["t2"] = AR.alloc([512], F32)
                Wm["sq"] = AR.alloc([512], BF16)
                Wm["STb"] = AR.alloc([2, 128], BF16)
                Wm["stg"] = AR.alloc([64], F32)
                Wm["stgT"] = AR.alloc([128], F32)
                Wm["eexpk"] = AR.alloc([128], F32)
                Wm["eaxl"] = AR.alloc([16], F32)
                Wm["t1s"] = AR.alloc([128], F32)
                Wm["t2s"] = AR.alloc([128], F32)
            else:
                for k_ in ("raws", "cbm", "xdt", "Bblk", "S0Tb", "eax", "zs", "Rex", "Lb", "Mt", "t1", "t2", "sq", "STb", "stg", "stgT", "eexpk", "eaxl", "t1s", "t2s"):
                    Wm[k_] = dummy
            assert AR.off <= (AR_WORDS if limit is None else limit), AR.off
            return Wm

        def rounds(nt):
            return [list(range(r0, min(nt, r0 + 4))) for r0 in range(0, nt, 4)]

        def mamba(ph, hT, hkey, rm, dual=False, pre_dt=None):
            M = ph == "M"
            wp = "mp" if dual else "all"
            BMAP = {0: 4, 1: 5, 2: 6, 3: 6, 6: 7} if dual else {}

            def mbk(b):
                return BMAP.get(b, b)
            N = NM if M else TP
            nt = 9 if M else 8
            blocks = [(0, 512), (512, 512)] + ([(1024, 128)] if M else [])
            Wm = carve_mamba(9216, 18432) if dual else carve_mamba()
            acumT, R1, R2 = Wm["acumT"], Wm["R1"], Wm["R2"]
            dt_tok, dA_tok, w_tok, eab, alastT = Wm["dt_tok"], Wm["dA_tok"], Wm["w_tok"], Wm["eab"], Wm["alastT"]
            BcT, CcT, xcT, Btok, cbm, xdt, xdtd = Wm["BcT"], Wm["CcT"], Wm["xcT"], Wm["Btok"], Wm["cbm"], Wm["xdt"], Wm["xdtd"]
            rawm = R1[:, 0:1027]
            raws = Wm["raws"]
            raws3 = raws[:, 0:176].rearrange("p (s k) -> p s k", k=11)
            ssqacc = R2
            S0h = R1[:, 0:1024].rearrange("p (s n) -> p s n", n=128)

            def hk(blk):
                return [(hkey, t) for t in range(blk[0] // 128, (blk[0] + blk[1]) // 128)]

            if pre_dt is not None:
                s_dt = pre_dt
            else:
                s_dt = wslot(wp)
                P.dma("pool", "d_w%d" % s_dt, lambda e: e.dma_start(out=wpool[:, s_dt, :, 0:32], in_=w_in[:, ODT:ODT + 32].rearrange("(c p) n -> p c n", p=128)), w=[("w", s_dt)])
            for i, blk in enumerate(blocks):
                bi = mbk(i % 2)
                c0, n = blk

                def f(e, bi=bi, c0=c0, n=n):
                    for c in range(16):
                        ins = e.matmul(psum[0:32, bi * 512:bi * 512 + n], lhsT=wpool[:, s_dt, c, 0:32], rhs=hT[:, c, c0:c0 + n], start=(c == 0), stop=(c == 15))
                    return ins
                P.pe(f, r=[("w", s_dt)] + hk(blk), w=[("ps", bi)])
                P.act(lambda e, bi=bi, c0=c0, n=n: e.activation(out=R1[0:32, c0:c0 + n], in_=psum[0:32, bi * 512:bi * 512 + n], func=AF.Exp, bias=cols[0:32, C_DTB:C_DTB + 1]), r=[("ps", bi)] + CK, w=["R1"])
            P.act(lambda e: e.activation(out=R1[0:32, 0:N], in_=R1[0:32, 0:N], func=AF.Ln, bias=1.0), r=["R1"], w=["R1"])
            Acol = small[0:32, 8:9]
            P.act(lambda e: e.activation(out=Acol, in_=cols[0:32, C_ALOG:C_ALOG + 1], func=AF.Exp), r=CK, w=["Acol"])
            P.dve(lambda e: e.tensor_scalar(out=Acol, in0=Acol, scalar1=-1.0, scalar2=None, op0=ALU.mult), r=["Acol"], w=["Acol"])
            P.dve(lambda e: e.tensor_scalar(out=R2[0:32, 0:N], in0=R1[0:32, 0:N], scalar1=Acol, scalar2=None, op0=ALU.mult), r=["R1", "Acol"], w=["R2"])
            P.dve(lambda e: e.tensor_tensor_scan(out=acumT[0:32, 0:N], data0=rm[0:32, 0:N], data1=R2[0:32, 0:N], initial=0.0, op0=ALU.mult, op1=ALU.add), r=["R2", "rm"], w=["acumT"])

            def tr32(src, dst, ks, kd):
                def f(e):
                    for t in range(nt):
                        ins = e.transpose(out=bank(mbk(2), 32, t * 32), in_=src[0:32, t * 128:(t + 1) * 128], identity=identf[0:32, 0:32])
                    return ins
                P.pe(f, r=[ks] + CK, w=[("ps", mbk(2))])
                P.act(lambda e: e.copy(out=dst[:, 0:nt * 32], in_=bank(mbk(2), nt * 32)), r=[("ps", mbk(2))], w=[kd])
            tr32(R1, dt_tok, "R1", "dt_tok")
            tr32(R2, dA_tok, "R2", "dA_tok")
            acm = acumT[0:32, 0:1024].rearrange("p (t k) -> p t k", k=128)
            P.dve(lambda e: e.tensor_copy(out=alastT[0:32, 0:8], in_=acm[:, :, 127]), r=["acumT"], w=["alastT"])
            P.dve(lambda e: e.tensor_tensor(out=R2[0:32, 0:1024].rearrange("p (t k) -> p t k", k=128), in0=alastT[0:32, 0:8].unsqueeze(2).to_broadcast([32, 8, 128]), in1=acm, op=ALU.subtract), r=["acumT", "alastT"], w=["R2"])
            if M:
                acs = acumT[0:32, 1024:1152].rearrange("p (s k) -> p s k", k=8)
                P.dve(lambda e: e.tensor_copy(out=alastT[0:32, 8:24], in_=acs[:, :, 7]), r=["acumT"], w=["alastT"])
                P.dve(lambda e: e.tensor_tensor(out=R2[0:32, 1024:1152].rearrange("p (s k) -> p s k", k=8), in0=alastT[0:32, 8:24].unsqueeze(2).to_broadcast([32, 16, 8]), in1=acs, op=ALU.subtract), r=["acumT", "alastT"], w=["R2"])
            P.act(lambda e: e.activation(out=R2[0:32, 0:N], in_=R2[0:32, 0:N], func=AF.Exp), r=["R2"], w=["R2"])
            P.dve(lambda e: e.tensor_tensor(out=R2[0:32, 0:N], in0=R2[0:32, 0:N], in1=R1[0:32, 0:N], op=ALU.mult), r=["R2", "R1"], w=["R2"])
            tr32(R2, w_tok, "R2", "w_tok")
            P.pe(lambda e: e.matmul(bank(mbk(3), 256), lhsT=onesf[:], rhs=dA_tok[:, 0:256], start=True, stop=True), r=["dA_tok"] + CK, w=[("ps", mbk(3))])
            P.act(lambda e: e.activation(out=eab[:, 0:256], in_=bank(mbk(3), 256), func=AF.Exp), r=[("ps", mbk(3))], w=["eab"])

            def proj_conv(cb, slot, dst, dkey):
                cwc = [cols[:, C_CW + j * 24 + cb:C_CW + j * 24 + cb + 1] for j in range(4)]
                cbc = cols[:, C_CB + cb:C_CB + cb + 1]
                if M:
                    P.dve(lambda e: e.tensor_copy(out=rawm[:, 0:3], in_=convh[:, cb, :]), r=["convh"], w=["R1"])
                    P.dma("sp", "d_sc", lambda e: e.dma_start(out=Wm["stgT"][0:48, :], in_=sc_d.rearrange("s j c -> (s j) c")[:, cb * 128:(cb + 1) * 128]), w=["stgT"])
                    P.pe(lambda e: e.transpose(out=bank(1, 48), in_=Wm["stgT"][0:48, :], identity=identf[0:48, 0:48]), r=["stgT"] + CK, w=[("ps", 1)])
                    P.act(lambda e: e.copy(out=raws3[:, :, 0:3], in_=bank(1, 48).rearrange("p (s k) -> p s k", k=3)), r=[("ps", 1)], w=["raws"])
                else:
                    P.pool(lambda e: e.memset(rawm[:, 0:3], 0.0), w=["R1"])
                for i, blk in enumerate(blocks):
                    bi = mbk(i % 2)
                    c0, n = blk

                    def f(e, bi=bi, c0=c0, n=n):
                        for c in range(16):
                            ins = e.matmul(bank(bi, n), lhsT=wpool[:, slot, c, :], rhs=hT[:, c, c0:c0 + n], start=(c == 0), stop=(c == 15))
                        return ins
                    P.pe(f, r=[("w", slot)] + hk(blk), w=[("ps", bi)])
                    if c0 < 1024:
                        P.act(lambda e, bi=bi, c0=c0, n=n: e.copy(out=rawm[:, 3 + c0:3 + c0 + n], in_=bank(bi, n)), r=[("ps", bi)], w=["R1"])
                    else:
                        P.act(lambda e, bi=bi: e.copy(out=raws3[:, :, 3:11], in_=bank(bi, 128).rearrange("p (s k) -> p s k", k=8)), r=[("ps", bi)], w=["raws"])
                for ci_, (c0, n) in enumerate([(0, 512), (512, 512)]):
                    ci = mbk(ci_)
                    acc = bank(ci, n)
                    P.dve(lambda e, acc=acc, c0=c0, n=n: e.tensor_scalar(out=acc, in0=rawm[:, c0:c0 + n], scalar1=cwc[0], scalar2=cbc, op0=ALU.mult, op1=ALU.add), r=["R1"] + CK, w=[("ps", ci)])
                    for j in range(1, 4):
                        P.dve(lambda e, acc=acc, c0=c0, n=n, j=j: e.scalar_tensor_tensor(out=acc, in0=rawm[:, c0 + j:c0 + j + n], scalar=cwc[j], in1=acc, op0=ALU.mult, op1=ALU.add), r=["R1", ("ps", ci)] + CK, w=[("ps", ci)])
                    P.act(lambda e, acc=acc, c0=c0, n=n: e.activation(out=dst[:, c0:c0 + n], in_=acc, func=AF.Silu), r=[("ps", ci)], w=[dkey])
                if M:
                    accs = bank(0, 176)
                    P.dve(lambda e: e.tensor_scalar(out=accs[:, 0:173], in0=raws[:, 0:173], scalar1=cwc[0], scalar2=cbc, op0=ALU.mult, op1=ALU.add), r=["raws"] + CK, w=[("ps", 0)])
                    for j in range(1, 4):
                        P.dve(lambda e, j=j: e.scalar_tensor_tensor(out=accs[:, 0:173], in0=raws[:, j:j + 173], scalar=cwc[j], in1=accs[:, 0:173], op0=ALU.mult, op1=ALU.add), r=["raws", ("ps", 0)] + CK, w=[("ps", 0)])
                    P.act(lambda e: e.activation(out=dst[:, 1024:1152].rearrange("p (s k) -> p s k", k=8), in_=accs.rearrange("p (s k) -> p s k", k=11)[:, :, 0:8], func=AF.Silu), r=[("ps", 0)], w=[dkey])
                    stg, stgT = Wm["stg"], Wm["stgT"]
                    P.dve(lambda e: e.tensor_copy(out=stg[:, 0:48].rearrange("p (s k) -> p s k", k=3), in_=raws3[:, :, 8:11]), r=["raws"], w=["stg"])
                    P.dve(lambda e: e.tensor_copy(out=stg[:, 48:51], in_=rawm[:, 1024:1027]), r=["R1"], w=["stg"])
                    P.pe(lambda e: e.transpose(out=psum[0:51, 512:512 + 128], in_=stg[:, 0:51], identity=identf[:]), r=["stg"] + CK, w=[("ps", 1)])
                    P.act(lambda e: e.copy(out=stgT[0:51, :], in_=psum[0:51, 512:512 + 128]), r=[("ps", 1)], w=["stgT"])
                    P.dma("sp", "d_cvo", lambda e: e.dma_start(out=conv_s.rearrange("s j c -> (s j) c")[:, cb * 128:(cb + 1) * 128], in_=stgT[0:48, :]), r=["stgT"])
                    P.dma("sp", "d_cvo", lambda e: e.dma_start(out=conv_p[:, cb * 128:(cb + 1) * 128], in_=stgT[48:51, :]), r=["stgT"])
                else:
                    P.dve(lambda e: e.tensor_copy(out=convh[:, cb, :], in_=rawm[:, 1024:1027]), r=["R1"], w=["convh"])

            shf = Shg[:].rearrange("p a b -> p (a b)")
            H1 = {"xcT": shf[:, 0:576].bitcast(BF16), "zs": shf[:, 576:1152].bitcast(BF16), "eexpk": shf[:, 1152:1280], "eaxl": shf[:, 1280:1296],
                  "Xq": shf[:, 1296:1808].rearrange("p (s n) -> p s n", n=128), "stgF": shf[:, 1808:1936]}

            def make_group(g):
                def GL_B():
                    sB = wfetch(w_in[:, OB + g * 128:OB + (g + 1) * 128], wp)
                    proj_conv(16 + g, sB, BcT, "BcT")

                def GL_rest():
                    sC = wfetch(w_in[:, OC + g * 128:OC + (g + 1) * 128], wp)
                    proj_conv(20 + g, sC, CcT, "CcT")
                    for tiles in rounds(nt):
                        r0, n_ = tiles[0], len(tiles)

                        def ftb(e, tiles=tiles):
                            for j, t in enumerate(tiles):
                                ins = e.transpose(out=bankb(mbk(2))[:, j * 128:(j + 1) * 128], in_=BcT[:, t * 128:(t + 1) * 128], identity=identb[:])
                            return ins
                        P.pe(ftb, r=["BcT"] + CK, w=[("ps", mbk(2))])
                        P.act(lambda e, r0=r0, n_=n_: e.copy(out=Btok[:, r0:r0 + n_, :], in_=bankb(mbk(2))[:, 0:n_ * 128].rearrange("p (t k) -> p t k", k=128)), r=[("ps", mbk(2))], w=["Btok"])
                        if M:
                            def fcb(e, tiles=tiles):
                                for j, t in enumerate(tiles):
                                    ins = e.matmul(bank(3, 128, j * 128), lhsT=BcT[:, t * 128:(t + 1) * 128], rhs=CcT[:, t * 128:(t + 1) * 128], start=True, stop=True)
                                return ins
                            P.pe(fcb, r=["BcT", "CcT"], w=[("ps", 3)])
                            mk = maskb if r0 == 8 else maskc
                            P.dve(lambda e, r0=r0, n_=n_, mk=mk: e.tensor_tensor(out=cbm[:, r0:r0 + n_, :], in0=bank(3, n_ * 128).rearrange("p (t k) -> p t k", k=128), in1=mk[:].unsqueeze(1).to_broadcast([128, n_, 128]), op=ALU.mult), r=[("ps", 3)] + CK, w=["cbm"])

                def blk_vars(kk, hs):
                    k = g * 4 + kk
                    if hs == 0:
                        return k, xcT, Wm["zs"], Wm["eexpk"], Wm["eaxl"], ("xcT", 0), ("zs", 0), ("eexpk", 0), ("eaxl", 0)
                    if not M:
                        return k, Wm["xcT2"], Wm["zs"], Wm["eexpk"], Wm["eaxl"], ("xcT", 1), ("zs", 0), ("eexpk", 0), ("eaxl", 0)
                    return k, H1["xcT"], H1["zs"], H1["eexpk"], H1["eaxl"], ("xcT", 1), ("zs", 1), ("eexpk", 1), ("eaxl", 1)

                def blk_early(kk, hs):
                    k, xcT_, zs, eexpk, eaxl_, KX, KZ, KE, KA = blk_vars(kk, hs)
                    sx = wfetch(w_in[:, OX + k * 128:OX + (k + 1) * 128], wp)
                    if M:
                        sz = wfetch(w_in[:, OZ + k * 128:OZ + (k + 1) * 128], wp)
                    proj_conv(k, sx, xcT_, KX)
                    if M:
                        P.dma("sp", "d_ee%d" % hs, lambda e: e.dma_start(out=eexpk[0:32, :], in_=eexp_d[:, k * 128:(k + 1) * 128]), w=[KE])
                        for i, (c0, n) in enumerate(blocks):
                            bz = (0, 1)[i % 2]

                            def fz(e, c0=c0, n=n, bz=bz):
                                for c in range(16):
                                    ins = e.matmul(bank(bz, n), lhsT=wpool[:, sz, c, :], rhs=hT[:, c, c0:c0 + n], start=(c == 0), stop=(c == 15))
                                return ins
                            P.pe(fz, r=[("w", sz)] + hk((c0, n)), w=[("ps", bz)])
                            P.act(lambda e, c0=c0, n=n, bz=bz: e.activation(out=zs[:, c0:c0 + n], in_=bank(bz, n), func=AF.Silu), r=[("ps", bz)], w=[KZ])
                        P.pe(lambda e: e.matmul(bank(1, 16), lhsT=eexpk[0:32, :], rhs=alastT[0:32, 8:24], start=True, stop=True), r=[KE, "alastT"], w=[("ps", 1)])
                        P.act(lambda e: e.activation(out=eaxl_[:, 0:16], in_=bank(1, 16), func=AF.Exp), r=[("ps", 1)], w=[KA])


                def blk_late(kk, hs):
                    k, xcT_, zs, eexpk, eaxl_, KX, KZ, KE, KA = blk_vars(kk, hs)
                    SK = ("Sssm", k)
                    Sblk = Sssm[:, k * 128:(k + 1) * 128]
                    def s0q(qd):
                        if qd in (0, 3):
                            return H1["Xq"], "Xq", ["Xq"]
                        return (Wm["t1"], Wm["t2"])[qd - 1].rearrange("p (s n) -> p s n", n=128), ("t1", "t2")[qd - 1], [("t1", "t2")[qd - 1]]

                    def s0_load(qd):
                        S0q, QK, QW = s0q(qd)
                        sq0 = qd * 4
                        P.dma("sp", "d_s0m%d" % qd, lambda e: e.dma_start(out=S0q, in_=ss_d[sq0:sq0 + 4, 2 * k:2 * k + 2].rearrange("s h p n -> (h p) s n")), w=QW)
                    if M:
                        s0_load(0)

                    def stage_a1(t):
                        rt = t % 2
                        Rex, Lb = Wm["Rex"][:, rt, :], Wm["Lb"][:, rt, :]
                        db = 3
                        P.pool(lambda e: e.tensor_tensor(out=Rex.rearrange("p (h k) -> p h k", k=128), in0=dA_tok[:, t * 32 + 2 * k:t * 32 + 2 * k + 2].unsqueeze(2).to_broadcast([128, 2, 128]), in1=tri[:].unsqueeze(1).to_broadcast([128, 2, 128]), op=ALU.mult), r=["dA_tok"] + CK, w=[("Rex", rt)])
                        P.pe(lambda e: e.matmul(bank(db, 256), lhsT=ustr[:], rhs=Rex, start=True, stop=True), r=[("Rex", rt)] + CK, w=[("ps", db)])
                        P.act(lambda e: e.activation(out=Lb, in_=bank(db, 256), func=AF.Exp), r=[("ps", db)], w=[("Lb", rt)])

                    def stage_a2(t):
                        rt = t % 2
                        Lb, Mt = Wm["Lb"][:, rt, :], Wm["Mt"][:, rt]
                        P.dve(lambda e: e.tensor_tensor(out=Mt, in0=Lb.rearrange("p (h k) -> p h k", k=128), in1=cbm[:, t, :].unsqueeze(1).to_broadcast([128, 2, 128]), op=ALU.mult), r=[("Lb", rt), "cbm"], w=[("Mt", rt)])

                    def stage_b(j, t):
                        rt = t % 2
                        Mt = Wm["Mt"][:, rt]
                        if M:
                            def fyi(e):
                                for hh in range(2):
                                    ins = e.matmul(psum[hh * 64:(hh + 1) * 64, 4 * 512 + j * 128:4 * 512 + (j + 1) * 128], lhsT=xdt[:, t, hh * 64:(hh + 1) * 64], rhs=Mt[:, hh, :], start=True, stop=True)
                                return ins
                            P.pe(fyi, r=["xdt", ("Mt", rt)], w=[("ps", 4)])
                        if t < 8:
                            if M:
                                sl = t % 2
                                P.pool(lambda e: e.tensor_copy(out=Wm["STb"][:, sl, :], in_=Sblk), r=[SK], w=[("STb", sl)])
                                P.pe(lambda e: e.matmul(bank(5, 128, j * 128), lhsT=Wm["STb"][:, sl, :], rhs=CcT[:, t * 128:(t + 1) * 128], start=True, stop=True), r=[("STb", sl), "CcT"], w=[("ps", 5)])
                            P.pe(lambda e: e.matmul(bank(mbk(6), 128, j * 128), lhsT=Btok[:, t, :], rhs=xdtd[:, t, :], start=True, stop=True), r=["Btok", "xdtd"], w=[("ps", mbk(6))])
                            for hh in range(2):
                                P.dve(lambda e, hh=hh: e.scalar_tensor_tensor(out=Sblk[:, hh * 64:(hh + 1) * 64], in0=Sblk[:, hh * 64:(hh + 1) * 64], scalar=eab[:, t * 32 + 2 * k + hh:t * 32 + 2 * k + hh + 1], in1=bank(mbk(6), 64, j * 128 + hh * 64), op0=ALU.mult, op1=ALU.add), r=[SK, "eab", ("ps", mbk(6))], w=[SK])
                        else:
                            s0_load(1)
                            s0_load(2)
                            for qd in range(4):
                                sq0 = qd * 4
                                qb_ = qd
                                S0q, QK, QW = s0q(qd)

                                def ftS(e, S0q=S0q):
                                    for q in range(4):
                                        ins = e.transpose(out=bank(3, 128, q * 128), in_=S0q[:, q, :], identity=identf[:])
                                    return ins
                                P.pe(ftS, r=[QK] + CK, w=[("ps", 3)])
                                P.act(lambda e: e.copy(out=Wm["S0Tb"], in_=bank(3, 512).rearrange("p (s n) -> p s n", n=128)), r=[("ps", 3)], w=["S0Tb"])

                                def fyis(e, sq0=sq0):
                                    for q in range(4):
                                        sidx = sq0 + q
                                        ins = e.matmul(bank(5, 8, sidx * 8), lhsT=Wm["S0Tb"][:, q, :], rhs=CcT[:, 1024 + sidx * 8:1024 + sidx * 8 + 8], start=True, stop=True)
                                    return ins
                                P.pe(fyis, r=["S0Tb", "CcT"], w=[("ps", 5)])
                                P.dve(lambda e, sq0=sq0: e.tensor_tensor(out=Wm["Bblk"], in0=Btok[:, 8, :].unsqueeze(1).to_broadcast([128, 4, 128]), in1=bmask[:, sq0:sq0 + 4].unsqueeze(2).to_broadcast([128, 4, 128]), op=ALU.mult), r=["Btok"] + CK, w=["Bblk"])
                                P.pe(lambda e: e.matmul(bank(2, 512), lhsT=xdtd[:, 8, :], rhs=Wm["Bblk"], start=True, stop=True), r=["xdtd", "Bblk"], w=[("ps", 2)])
                                P.dve(lambda e, sq0=sq0, S0q=S0q: e.tensor_tensor(out=S0q, in0=S0q, in1=eaxl_[:, sq0:sq0 + 4].unsqueeze(2).to_broadcast([128, 4, 128]), op=ALU.mult), r=[QK, KA], w=[QK])
                                P.dve(lambda e, S0q=S0q: e.tensor_tensor(out=S0q, in0=S0q, in1=bank(2, 512).rearrange("p (s n) -> p s n", n=128), op=ALU.add), r=[QK, ("ps", 2)], w=[QK])
                                P.dma("sp", "d_s0o%d" % qb_, lambda e, sq0=sq0, S0q=S0q: e.dma_start(out=ssm_s[sq0:sq0 + 4, 2 * k:2 * k + 2].rearrange("s h p n -> (h p) s n"), in_=S0q), r=[QK])
                                if qd == 0:
                                    s0_load(3)

                    if M:
                        stage_a1(0)
                        stage_a1(1)
                        stage_a2(0)
                    for tiles in rounds(nt):
                        r0, n_ = tiles[0], len(tiles)
                        n = n_ * 128
                        c0 = r0 * 128

                        def ftx(e, tiles=tiles):
                            for j, t in enumerate(tiles):
                                ins = e.transpose(out=bankb(mbk(2))[:, j * 128:(j + 1) * 128], in_=xcT_[:, t * 128:(t + 1) * 128], identity=identb[:])
                            return ins
                        P.pe(ftx, r=[KX] + CK, w=[("ps", mbk(2))])
                        xps = bankb(mbk(2))[:, 0:n].rearrange("p (t h q) -> p t h q", h=2, q=64)
                        P.dve(lambda e, r0=r0, n_=n_, xps=xps: e.tensor_tensor(out=xdtd[:, r0:r0 + n_, :].rearrange("p t (h q) -> p t h q", q=64), in0=xps, in1=w_tok.rearrange("p (t h) -> p t h", h=32)[:, r0:r0 + n_, 2 * k:2 * k + 2].unsqueeze(3).to_broadcast([128, n_, 2, 64]), op=ALU.mult), r=[("ps", mbk(2)), "w_tok"], w=["xdtd"])
                        if M:
                            P.dve(lambda e, r0=r0, n_=n_, xps=xps: e.tensor_tensor(out=xdt[:, r0:r0 + n_, :].rearrange("p t (h q) -> p t h q", q=64), in0=xps, in1=dt_tok.rearrange("p (t h) -> p t h", h=32)[:, r0:r0 + n_, 2 * k:2 * k + 2].unsqueeze(3).to_broadcast([128, n_, 2, 64]), op=ALU.mult), r=[("ps", 2), "dt_tok"], w=["xdt"])
                            P.pe(lambda e, c0=c0, n=n: e.matmul(bank(7, n), lhsT=eexpk[0:32, :], rhs=acumT[0:32, c0:c0 + n], start=True, stop=True), r=[KE, "acumT"], w=[("ps", 7)])
                            P.act(lambda e, n=n: e.activation(out=Wm["eax"][:, 0:n], in_=bank(7, n), func=AF.Exp), r=[("ps", 7)], w=["eax"])
                        for j, t in enumerate(tiles):
                            if M and t + 2 < nt:
                                stage_a1(t + 2)
                            if M and t + 1 < nt:
                                stage_a2(t + 1)
                            stage_b(j, t)
                        if M and r0 == 4:
                            P.pe(lambda e: e.transpose(out=bank(3, 128), in_=Sblk, identity=identf[:]), r=[SK] + CK, w=[("ps", 3)])
                            P.act(lambda e: e.copy(out=H1["stgF"], in_=bank(3, 128)), r=[("ps", 3)], w=["stgF"])
                            P.dma("sp", "d_ssp", lambda e: e.dma_start(out=ssm_p[2 * k:2 * k + 2].rearrange("h p n -> (h p) n"), in_=H1["stgF"]), r=["stgF"])
                        if M:
                            sq_ = Wm["sq"]
                            if r0 == 8:
                                t1, t2, k1, k2 = Wm["t1s"], Wm["t2s"], "t1s", "t2s"
                            else:
                                t1, t2, k1, k2 = Wm["t1"], Wm["t2"], "t1", "t2"
                            P.dve(lambda e, n=n, t1=t1: e.tensor_tensor(out=t1[:, 0:n], in0=bank(5, n), in1=Wm["eax"][:, 0:n], op=ALU.mult), r=[("ps", 5), "eax"], w=[k1])
                            P.dve(lambda e, n=n, t1=t1, t2=t2: e.tensor_tensor(out=t2[:, 0:n], in0=t1[:, 0:n], in1=bank(4, n), op=ALU.add), r=[k1, ("ps", 4)], w=[k2])
                            P.dve(lambda e, n=n, c0=c0, t1=t1, t2=t2: e.scalar_tensor_tensor(out=t1[:, 0:n], in0=xcT_[:, c0:c0 + n], scalar=cols[:, C_DSK + k:C_DSK + k + 1], in1=t2[:, 0:n], op0=ALU.mult, op1=ALU.add), r=[KX, k2] + CK, w=[k1])
                            P.dve(lambda e, n=n, c0=c0, t1=t1: e.tensor_tensor(out=yT[:, k, c0:c0 + n], in0=t1[:, 0:n], in1=zs[:, c0:c0 + n], op=ALU.mult), r=[k1, KZ], w=[("yT", k)])
                            P.act(lambda e, n=n, c0=c0: e.activation(out=sq_[:, 0:n], in_=yT[:, k, c0:c0 + n], func=AF.Square), r=[("yT", k)], w=["sq"])
                            P.pe(lambda e, n=n: e.matmul(bank(7, n), lhsT=onesb[:], rhs=sq_[:, 0:n], start=True, stop=True), r=["sq"] + CK, w=[("ps", 7)])
                            if kk == 0:
                                P.dve(lambda e, n=n, c0=c0: e.tensor_copy(out=ssqacc[:, c0:c0 + n], in_=bank(7, n)), r=[("ps", 7)], w=["R2"])
                            else:
                                P.dve(lambda e, n=n, c0=c0: e.tensor_tensor(out=ssqacc[:, c0:c0 + n], in0=ssqacc[:, c0:c0 + n], in1=bank(7, n), op=ALU.add), r=[("ps", 7), "R2"], w=["R2"])

                def gnorm():
                    if M:
                        P.act(lambda e: e.activation(out=ssqacc[:, 0:N], in_=ssqacc[:, 0:N], func=AF.Ln, scale=1.0 / 512, bias=epsc), r=["R2"], w=["R2"])
                        P.act(lambda e: e.activation(out=ssqacc[:, 0:N], in_=ssqacc[:, 0:N], func=AF.Exp, scale=-0.5), r=["R2"], w=["R2"])
                        for kk in range(4):
                            k = g * 4 + kk
                            P.dve(lambda e, k=k: e.scalar_tensor_tensor(out=yT[:, k, :], in0=yT[:, k, :], scalar=cols[:, C_SSMN + k:C_SSMN + k + 1], in1=ssqacc[:, 0:N], op0=ALU.mult, op1=ALU.mult), r=[("yT", k), "R2"] + CK, w=[("yT", k)])

                return dict(GL_B=GL_B, GL_rest=GL_rest, early=blk_early, late=blk_late, norm=gnorm)

            G = [make_group(g_) for g_ in range(4)]
            if M:
                G[0]["GL_B"]()
                G[0]["GL_rest"]()
                G[0]["early"](0, 0)
                for g_ in range(4):
                    for kk_ in range(3):
                        late_ops = P.capture(lambda: G[g_]["late"](kk_, kk_ % 2))
                        early_ops = P.capture(lambda: G[g_]["early"](kk_ + 1, (kk_ + 1) % 2))
                        P.interleave(early_ops, late_ops)
                    if g_ < 3:
                        late_ops = P.capture(lambda: G[g_]["late"](3, 1))
                        early_ops = P.capture(lambda: (G[g_ + 1]["GL_B"](), G[g_ + 1]["early"](0, 0)))
                        P.interleave(early_ops, late_ops)
                        G[g_]["norm"]()
                        G[g_ + 1]["GL_rest"]()
                    else:
                        G[g_]["late"](3, 1)
                        G[g_]["norm"]()
            else:
                for g_ in range(4):
                    G[g_]["GL_B"]()
                    G[g_]["GL_rest"]()
                    if dual:
                        G[g_]["early"](0, 0)
                        for kk_ in range(4):
                            late_ops = P.capture(lambda: G[g_]["late"](kk_, kk_ % 2))
                            early_ops = P.capture(lambda: G[g_]["early"](kk_ + 1, (kk_ + 1) % 2)) if kk_ < 3 else []
                            P.interleave(early_ops, late_ops)
                    else:
                        for kk_ in range(4):
                            G[g_]["early"](kk_, 0)
                            G[g_]["late"](kk_, 0)
            return Wm

        def carve_common():
            AR.off = 0
            hT = AR.alloc([16, NM], BF16)
            oT_ = AR.alloc([16, NM], BF16)
            yT_ = AR.alloc([16, NM], BF16)
            return hT, oT_, yT_

        hT, oT, yT = carve_common()
        work_off = AR.off

        def carve_hgrn():
            AR.off = work_off
            shared = {}
            for k in ("T1", "T2", "lf", "bb"):
                shared[k] = AR.alloc([NM], F32)
            for k in ("kkb", "qb", "vT"):
                shared[k] = AR.alloc([NM], BF16)
            for k in ("ktok", "scm", "Sbf"):
                shared[k] = AR.alloc([9, 128], BF16)
            shared["osq"] = AR.alloc([512], BF16)
            for k in ("rs", "on"):
                shared[k] = AR.alloc([512], F32)
            shared["blast"] = AR.alloc([24], F32)
            shared["bref"] = AR.alloc([16], F32)
            hand = ("kl", "km", "qm", "qn", "gs")
            sets = []
            for wi in range(2):
                if wi == 1:
                    AR.off = 18432
                Wd = dict(shared)
                for k in hand:
                    Wd[k] = AR.alloc([NM], BF16)
                Wd["vtok"] = AR.alloc([9, 128], BF16)
                Wd["elast"] = AR.alloc([24], F32)
                if wi == 0:
                    assert AR.off <= AR_WORDS, AR.off
                ky = {k: k for k in shared}
                for k in hand + ("vtok", "elast"):
                    ky[k] = (k, wi)
                Wd["_k"] = ky
                sets.append(Wd)
            S0 = AR.alloc([16, 128], F32)
            S0b = AR.alloc([16, 128], BF16)
            Vb = AR.alloc([16, 128], BF16)
            assert AR.off <= 27648, AR.off
            for Wd in sets:
                Wd["S0"], Wd["S0b"], Wd["Vb"] = S0, S0b, Vb
                for k in ("S0", "S0b", "Vb"):
                    Wd["_k"][k] = k
            return sets

        WS = carve_hgrn()
        W = WS[0]
        sa_off = 16 * NM // 2
        sa = arena_t[:, sa_off:sa_off + 2 * 16 * NM // 2]
        xt2 = sa[:, 0:4096].rearrange("p (s d) -> p s d", d=2048)
        junk = sa[:, 4096:6144]
        xn = sa[:, 6144:7168].bitcast(BF16)
        xnB = sa[:, 7168:8192].bitcast(BF16)
        ssq = small[:, 0:2]
        rstd = small[:, 2:4]
        rmt = sb("rmt", [128, NM], BF16)

        run_P = debug.get("run_P", True)
        nheads = debug.get("nheads", 16)
        if run_P:
            P.dma("sp", "d_rm", lambda e: e.dma_start(out=rmt[:, 0:TP], in_=rmp_d), w=["rm"])
            def hwP(h):
                return {"f": wfetch(w_in[:, OF + h * 128:OF + (h + 1) * 128], "hp"), "v": wfetch(w_in[:, OV + h * 128:OV + (h + 1) * 128], "hp")}
            SLP0 = hwP(0)
            stage_A(xp_d, 8, hT, "hT", C_NMIX, xt2, xn, ssq, rstd, junk)
            P.barrier()
            bkP = dict(proj=[0, 1], v=2, tr=3, ds=3)

            def hgrn_P():
                slots = {0: SLP0}
                hgrn_head(0, "P", hT, "hT", WS[0], rmt, slots[0], "early", bkP)
                for h in range(nheads):
                    late = P.capture(lambda: hgrn_head(h, "P", hT, "hT", WS[h % 2], rmt, slots[h], "late", bkP))
                    early = []
                    if h + 1 < nheads:
                        slots[h + 1] = hwP(h + 1)
                        early = P.capture(lambda: hgrn_head(h + 1, "P", hT, "hT", WS[(h + 1) % 2], rmt, slots[h + 1], "early", bkP))
                    P.interleave(early, late)
            hg_ops = P.capture(hgrn_P)
            mb_ops = P.capture(lambda: mamba("P", hT, "hT", rmt, dual=True)) if debug.get("mamba", True) else []
            P.interleave(hg_ops, mb_ops)
            P.barrier()

        def hw(h):
            return {k: wfetch(w_in[:, o + h * 128:o + (h + 1) * 128]) for k, o in (("f", OF), ("v", OV), ("q", OQ), ("g", OG))}
        slots = {0: hw(0)}
        P.dma("sp", "d_rm", lambda e: e.dma_start(out=rmt[:, 0:NM], in_=rmm_d), w=["rm"])
        stage_A(xm_d, 9, hT, "hT", C_NMIX, xt2, xn, ssq, rstd, junk)
        P.barrier()
        P.pool(lambda e: e.memset(W["bref"][:], 0.0), w=["bref"])
        P.pool(lambda e: e.memset(W["scm"][64:128, :, 0:64], 0.0), w=["scm"])

        hgrn_head(0, "M", hT, "hT", WS[0], rmt, slots[0], "early")
        for h in range(nheads):
            late = P.capture(lambda: hgrn_head(h, "M", hT, "hT", WS[h % 2], rmt, slots[h], "late"))
            early = []
            if h + 1 < nheads:
                slots[h + 1] = hw(h + 1)
                early = P.capture(lambda: hgrn_head(h + 1, "M", hT, "hT", WS[(h + 1) % 2], rmt, slots[h + 1], "early"))
            P.interleave(early, late)
        PRE_DT = None
        if debug.get("mamba", True):
            PRE_DT = wslot("all")
            P.dma("pool", "d_w%d" % PRE_DT, lambda e: e.dma_start(out=wpool[:, PRE_DT, :, 0:32], in_=w_in[:, ODT:ODT + 32].rearrange("(c p) n -> p c n", p=128)), w=[("w", PRE_DT)])
        P.barrier()
        PRE2 = None
        if debug.get("mamba", True):
            P.dve(lambda e: e.tensor_scalar(out=Sssm[:], in0=Sssm[:], scalar1=cols[:, C_FLAG:C_FLAG + 1], scalar2=None, op0=ALU.mult), r=[("Sssm", k) for k in range(16)] + CK, w=[("Sssm", k) for k in range(16)])
            WmD = mamba("M", hT, "hT", rmt, pre_dt=PRE_DT)
            PRE2 = [wfetch(w_bhg[:, 0:128]), wfetch(w_bssm[:, 0:128]), wfetch(w_in[:, OG0:OG0 + 128]), wfetch(w_in[:, OG1:OG1 + 128])]
            P.barrier()
            for k_ in dbg_d:
                if k_.startswith("M_"):
                    P.dma("sp", "d_dbg", lambda e, k_=k_: e.dma_start(out=dbg_d[k_], in_=WmD[k_[2:]]), r=[])

        blocks3 = [(0, 512), (512, 512), (1024, 128)]
        OT_K = [("oT", h) for h in range(16)]
        YT_K = [("yT", k) for k in range(16)]
        HT_K = [("hT", t) for t in range(9)]
        AR.off = work_off
        mT = AR.alloc([16, NM], BF16)
        actT = AR.alloc([4, NM], BF16)
        tmpA = AR.alloc([512], F32)
        tmpB = AR.alloc([512], F32)
        xn2 = AR.alloc([2048], BF16)
        assert AR.off <= AR_WORDS, AR.off
        junk2 = actT.rearrange("p a b -> p (a b)")[:, 0:2048]
        AK = [("actT", fc) for fc in range(4)]
        x1 = arena_t[:, 9216:27648].rearrange("p (t d) -> p t d", d=2048)
        wb2 = arena_t[:, 0:8192].bitcast(BF16).rearrange("p (s n) -> p s n", s=2)
        nfin = arena_t[:, 0:2048]
        wb2state = {"n": 0}

        def wb2_fetch(src_ap, nchunk):
            s_ = wb2state["n"] % 2
            wb2state["n"] += 1
            dstv = wb2[:, s_, :].rearrange("p (c n) -> p c n", c=nchunk)
            srcv = src_ap.rearrange("(c p) n -> p c n", p=128)
            hc = nchunk // 2
            P.dma_multi("pool", "d_wb%d" % s_, [lambda e, h_=h_: e.dma_start(out=dstv[:, hc * h_:hc * (h_ + 1), :], in_=srcv[:, hc * h_:hc * (h_ + 1), :]) for h_ in range(2)], w=[("wb2", s_)])
            return s_

        def p2_fetch(e_):
            return [wfetch(w_bhg[:, e_ * 128:(e_ + 1) * 128]), wfetch(w_bssm[:, e_ * 128:(e_ + 1) * 128]),
                    wfetch(w_in[:, OG0 + e_ * 128:OG0 + (e_ + 1) * 128]), wfetch(w_in[:, OG1 + e_ * 128:OG1 + (e_ + 1) * 128])]

        def p2_chunk(e_, pre=None):
            sl_ = pre if pre is not None else p2_fetch(e_)
            srcs = [(sl_[0], oT, OT_K), (sl_[1], yT, YT_K), (sl_[2], hT, HT_K), (sl_[3], hT, HT_K)]
            for i, (c0, n) in enumerate(blocks3):
                bs = 4 * (i % 2)
                for q, (slot, src, sk) in enumerate(srcs):
                    def f(e, q=q, slot=slot, src=src, c0=c0, n=n, bs=bs):
                        for c in range(16):
                            ins = e.matmul(bank(bs + q, n), lhsT=wpool[:, slot, c, :], rhs=src[:, c, c0:c0 + n], start=(c == 0), stop=(c == 15))
                        return ins
                    P.pe(f, r=[("w", slot)] + sk, w=[("ps", bs + q)])
                P.act(lambda e, bs=bs, n=n: e.activation(out=tmpA[:, 0:n], in_=bank(bs + 2, n), func=AF.Sigmoid), r=[("ps", bs + 2)], w=["tmpA"])
                P.act(lambda e, bs=bs, n=n: e.activation(out=tmpB[:, 0:n], in_=bank(bs + 3, n), func=AF.Sigmoid), r=[("ps", bs + 3)], w=["tmpB"])
                P.dve(lambda e, bs=bs, n=n: e.tensor_tensor(out=tmpA[:, 0:n], in0=tmpA[:, 0:n], in1=bank(bs, n), op=ALU.mult), r=["tmpA", ("ps", bs)], w=["tmpA"])
                P.dve(lambda e, bs=bs, n=n: e.tensor_tensor(out=tmpB[:, 0:n], in0=tmpB[:, 0:n], in1=bank(bs + 1, n), op=ALU.mult), r=["tmpB", ("ps", bs + 1)], w=["tmpB"])
                P.dve(lambda e, n=n, c0=c0: e.tensor_tensor(out=mT[:, e_, c0:c0 + n], in0=tmpA[:, 0:n], in1=tmpB[:, 0:n], op=ALU.add), r=["tmpA", "tmpB"], w=[("mT", t_) for t_ in range(c0 // 128, (c0 + n) // 128)])

        if debug.get("dense", True):
            for e_ in range(16):
                p2_chunk(e_, PRE2 if e_ == 0 else None)
            P.barrier()
            for t in range(9):
                P.dma("sp", "d_x1_%d" % t, lambda e, t=t: e.dma_start(out=x1[:, t, :], in_=xm_d[t * 128:(t + 1) * 128, :]), w=[("x1", t)])

            def p3_col(cb):
                s_ = wb2_fetch(w_out[:, cb * 512:(cb + 1) * 512], 16)
                wv = wb2[:, s_, :].rearrange("p (c n) -> p c n", c=16)
                for t in range(9):
                    bk = (cb * 9 + t) % 4

                    def f(e, t=t, bk=bk):
                        for c in range(16):
                            ins = e.matmul(bank(bk, 512), lhsT=mT[:, c, t * 128:(t + 1) * 128], rhs=wv[:, c, :], start=(c == 0), stop=(c == 15))
                        return ins
                    P.pe(f, r=[("wb2", s_), ("mT", t)], w=[("ps", bk)])
                    P.dve(lambda e, t=t, bk=bk: e.tensor_tensor(out=x1[:, t, cb * 512:(cb + 1) * 512], in0=x1[:, t, cb * 512:(cb + 1) * 512], in1=bank(bk, 512), op=ALU.add), r=[("ps", bk), ("x1", t)], w=[("x1", t)])
                    if cb == 3:
                        norm_T(t, x1[:, t, :], [("x1", t)], (xn2, xn2B)[t % 2], ["xn2"] if t % 2 == 0 else AK[1:4], junk2, AK[0:2], h2T, "h2T", C_NFFN, extra_w=[("mT", t)])
            h2T = mT
            xn2B = actT.rearrange("p a b -> p (a b)")[:, 2048:4096]
            for cb in range(3):
                p3_col(cb)
            PRE4 = (wfetch(w_g[:, 0:128]), wfetch(w_u[:, 0:128]))
            p3_col(3)
            H2_K = [("h2T", t) for t in range(9)]

            def ffn_group(fg):
                s_d = wb2_fetch(w_d[fg * 512:(fg + 1) * 512, :], 4)
                wdv = wb2[:, s_d, :].rearrange("p (c n) -> p c n", c=4)
                for fc in range(4):
                    f0 = fg * 512 + fc * 128
                    if fg == 0 and fc == 0:
                        s_g, s_u = PRE4
                    else:
                        s_g = wfetch(w_g[:, f0:f0 + 128])
                        s_u = wfetch(w_u[:, f0:f0 + 128])
                    for i, (c0, n) in enumerate(blocks3):
                        bg, bu = 2 * (i % 2), 2 * (i % 2) + 1

                        def fgm(e, slot=s_g, bk=bg, c0=c0, n=n):
                            for c in range(16):
                                ins = e.matmul(bank(bk, n), lhsT=wpool[:, slot, c, :], rhs=h2T[:, c, c0:c0 + n], start=(c == 0), stop=(c == 15))
                            return ins
                        P.pe(fgm, r=[("w", s_g)] + [("h2T", t_) for t_ in range(c0 // 128, (c0 + n) // 128)], w=[("ps", bg)])

                        def fum(e, slot=s_u, bk=bu, c0=c0, n=n):
                            for c in range(16):
                                ins = e.matmul(bank(bk, n), lhsT=wpool[:, slot, c, :], rhs=h2T[:, c, c0:c0 + n], start=(c == 0), stop=(c == 15))
                            return ins
                        P.pe(fum, r=[("w", s_u)] + [("h2T", t_) for t_ in range(c0 // 128, (c0 + n) // 128)], w=[("ps", bu)])
                        P.act(lambda e, bg=bg, n=n: e.activation(out=tmpA[:, 0:n], in_=bank(bg, n), func=AF.Silu), r=[("ps", bg)], w=["tmpA"])
                        P.dve(lambda e, bu=bu, n=n, c0=c0, fc=fc: e.tensor_tensor(out=actT[:, fc, c0:c0 + n], in0=tmpA[:, 0:n], in1=bank(bu, n), op=ALU.mult), r=["tmpA", ("ps", bu)], w=[("actT", fc)])
                for t in range(9):
                    for cb in range(4):
                        bk = 4 + (t * 4 + cb) % 4

                        def fd(e, t=t, cb=cb, bk=bk):
                            for fc in range(4):
                                ins = e.matmul(bank(bk, 512), lhsT=actT[:, fc, t * 128:(t + 1) * 128], rhs=wdv[:, fc, cb * 512:(cb + 1) * 512], start=(fc == 0), stop=(fc == 3))
                            return ins
                        P.pe(fd, r=[("wb2", s_d)] + [("actT", fc) for fc in range(4)], w=[("ps", bk)])
                        P.dve(lambda e, t=t, cb=cb, bk=bk: e.tensor_tensor(out=x1[:, t, cb * 512:(cb + 1) * 512], in0=x1[:, t, cb * 512:(cb + 1) * 512], in1=bank(bk, 512), op=ALU.add), r=[("ps", bk), ("x1", t)], w=[("x1", t)])
            for fg in range(11):
                ffn_group(fg)
            P.barrier()
            P.dma("sp", "d_nf", lambda e: e.dma_start(out=nfin, in_=nfin_d), w=["nfin"])
            for t in range(9):
                sl = t % 2
                P.act(lambda e, t=t, sl=sl: e.activation(out=junk2, in_=x1[:, t, :], func=AF.Square, accum_out=ssq[:, sl:sl + 1]), r=[("x1", t)], w=AK + [("ssq", sl)])
                P.act(lambda e, sl=sl: e.activation(out=rstd[:, sl:sl + 1], in_=ssq[:, sl:sl + 1], func=AF.Sqrt, scale=1.0 / D, bias=EPS), r=[("ssq", sl)], w=[("rstd", sl)])
                P.dve(lambda e, sl=sl: e.reciprocal(out=rstd[:, sl:sl + 1], in_=rstd[:, sl:sl + 1]), r=[("rstd", sl)], w=[("rstd", sl)])
                P.dve(lambda e, t=t, sl=sl: e.scalar_tensor_tensor(out=x1[:, t, :], in0=x1[:, t, :], scalar=rstd[:, sl:sl + 1], in1=nfin, op0=ALU.mult, op1=ALU.mult), r=[("x1", t), ("rstd", sl), "nfin"], w=[("x1", t)])
                if t < 8:
                    P.dma("sp", "d_yo", lambda e, t=t: e.dma_start(out=y_main[t * 128:(t + 1) * 128, :], in_=x1[:, t, :]), r=[("x1", t)])
                else:
                    P.dma("sp", "d_yo", lambda e, t=t: e.dma_start(out=y_samp, in_=x1[:, t, :]), r=[("x1", t)])

        if "oT" in dbg_d:
            P.dma("sp", "d_dbg", lambda e: e.dma_start(out=dbg_d["oT"], in_=oT[:, 0:nheads, :]), r=[("oT", h) for h in range(16)])
        for k_ in dbg_d:
            if k_.startswith("W_"):
                P.dma("sp", "d_dbg", lambda e, k_=k_: e.dma_start(out=dbg_d[k_], in_=W[k_[2:]]), r=[])
        if "Sssm" in dbg_d:
            P.dma("sp", "d_dbg", lambda e: e.dma_start(out=dbg_d["Sssm"], in_=Sssm[:]), r=[("Sssm", k) for k in range(16)])
        if "yT" in dbg_d:
            P.dma("sp", "d_dbg", lambda e: e.dma_start(out=dbg_d["yT"], in_=yT), r=[("yT", k) for k in range(16)])
        if "hT" in dbg_d:
            P.dma("sp", "d_dbg", lambda e: e.dma_start(out=dbg_d["hT"], in_=hT), r=[("hT", t) for t in range(9)])
        P.emit(nc, es)
        print("prog stats", P.stats, "arena words used", AR.off)
    return nc


def host_inputs(inputs):
    f32 = np.float32
    x_prompt = np.asarray(inputs["x_prompt"], f32)
    x_sample = np.asarray(inputs["x_sample"], f32)
    sh = np.asarray(inputs["state_hgrn"], f32)[0]
    ss = np.asarray(inputs["state_ssm"], f32)[0]
    sc = np.asarray(inputs["state_conv"], f32)[0]

    def col16(v):
        return np.asarray(v, f32).reshape(16, 128).T

    cols = np.zeros((128, NCOLS), f32)
    cols[:, C_NMIX:C_NMIX + 16] = col16(inputs["norm_mix"][0])
    cols[:, C_L0:C_L0 + 16] = col16(inputs["hg_lb_logits"][0])
    cols[:, C_L1:C_L1 + 16] = col16(inputs["hg_lb_logits"][1])
    cols[:, C_HGN:C_HGN + 16] = col16(inputs["hg_norm"][0])
    cols[:, C_SSMN:C_SSMN + 16] = col16(inputs["ssm_norm"][0])
    cols[:, C_NFFN:C_NFFN + 16] = col16(inputs["norm_ffn"][0])
    cw = np.asarray(inputs["conv_w"], f32)[0]
    cols[:, C_CW:C_CW + 96] = cw.reshape(4, 24, 128).transpose(2, 0, 1).reshape(128, 96)
    cols[:, C_CB:C_CB + 24] = np.asarray(inputs["conv_b"], f32)[0].reshape(24, 128).T
    cols[:, C_DSK:C_DSK + 16] = np.repeat(np.asarray(inputs["d_skip"], f32)[0], 64).reshape(16, 128).T
    cols[0:32, C_DTB] = np.asarray(inputs["dt_bias"], f32)[0]
    cols[0:32, C_ALOG] = np.asarray(inputs["a_log"], f32)[0]

    bf = ml_dtypes.bfloat16
    ii = np.arange(128)
    identf = np.eye(128, dtype=f32)
    maskc = (ii[:, None] <= ii[None, :]).astype(f32)
    maskb = maskc * ((ii[:, None] // 8) == (ii[None, :] // 8)).astype(f32)
    rmp = np.ones((128, TP), f32)
    rmp[:, 0::128] = 0.0
    rmm = np.ones((128, NM), f32)
    rmm[:, 0:1024:128] = 0.0
    rmm[:, 1024::8] = 0.0
    bmask = ((ii[:, None] // 8) == np.arange(16)[None, :]).astype(f32)
    ustr = (ii[:, None] > ii[None, :]).astype(f32)
    tri = (ii[:, None] <= ii[None, :]).astype(f32)
    eexp = np.repeat(np.eye(32, dtype=f32), 64, axis=1)
    shared = dict(
        w_in=np.ascontiguousarray(inputs["w_in"][0], f32), w_bhg=np.ascontiguousarray(inputs["w_branch_hg"][0], f32),
        w_bssm=np.ascontiguousarray(inputs["w_branch_ssm"][0], f32), w_out=np.ascontiguousarray(inputs["w_out"][0], f32),
        w_g=np.ascontiguousarray(inputs["w_ffn_gate"][0], f32), w_u=np.ascontiguousarray(inputs["w_ffn_up"][0], f32),
        w_d=np.ascontiguousarray(inputs["w_ffn_down"][0], f32),
        nfin=np.ascontiguousarray(np.broadcast_to(np.asarray(inputs["norm_final"], f32)[None, :], (128, D))),
        identb=identf.astype(bf), identf=identf, maskc=maskc.astype(bf), maskb=maskb.astype(bf),
        onesf=np.ones((128, 128), f32), onesb=np.ones((128, 128), f32).astype(bf), rmp=rmp.astype(bf), rmm=rmm.astype(bf), bmask=bmask.astype(bf), ustr=ustr, tri=tri, eexp=eexp,
    )
    maps = []
    for c in range(NCORES):
        b, hf = c // 2, c % 2
        m = dict(shared)
        m["xp"] = np.ascontiguousarray(x_prompt[b, 0:1024]) if hf == 1 else np.zeros((TP, D), f32)
        m["xm"] = np.ascontiguousarray(np.concatenate([x_prompt[b, hf * 1024:(hf + 1) * 1024], x_sample[16 * c:16 * c + 16].reshape(128, D)], 0))
        m["sh"] = np.ascontiguousarray(sh[16 * c:16 * c + 16])
        m["ss"] = np.ascontiguousarray(ss[16 * c:16 * c + 16])
        m["sc"] = np.ascontiguousarray(sc[16 * c:16 * c + 16])
        cc = cols.copy()
        cc[:, C_FLAG] = float(hf)
        m["cols"] = cc
        maps.append(m)
    return maps


_NC_CACHE = {}


def kernel(**inputs):
    maps = host_inputs(inputs)
    if "nc" not in _NC_CACHE:
        _NC_CACHE["nc"] = build()
    nc = _NC_CACHE["nc"]
    res = run_bass_kernel_spmd(nc, maps, core_ids=list(range(NCORES)))
    R = res.results
    f32 = np.float32
    y_prompt = np.zeros((4, 2048, D), f32)
    y_sample = np.zeros((128, 8, D), f32)
    hgp = np.zeros((1, 4, 16, 128, 128), f32)
    ssp = np.zeros((1, 4, 32, 64, 128), f32)
    cvp = np.zeros((1, 4, 3, 3072), f32)
    hgs = np.zeros((1, 128, 16, 128, 128), f32)
    sss = np.zeros((1, 128, 32, 64, 128), f32)
    cvs = np.zeros((1, 128, 3, 3072), f32)
    for c in range(NCORES):
        b, hf = c // 2, c % 2
        r = R[c]
        y_prompt[b, hf * 1024:(hf + 1) * 1024] = r["y_main"]
        y_sample[16 * c:16 * c + 16] = r["y_samp"].reshape(16, 8, D)
        if hf == 1:
            hgp[0, b] = r["hg_p"]
            ssp[0, b] = r["ssm_p"]
            cvp[0, b] = r["conv_p"]
        hgs[0, 16 * c:16 * c + 16] = r["hg_s"]
        sss[0, 16 * c:16 * c + 16] = r["ssm_s"]
        cvs[0, 16 * c:16 * c + 16] = r["conv_s"]
    return (y_prompt, y_sample, hgp, ssp, cvp, hgs, sss, cvs)
```
